# Optimizing a Trainium2 kernel written in Bass

```python
import jax, jax.numpy as jnp
from jax import lax
import numpy as np

D_MODEL = 1024
BATCH = 2
SEQ = 8192
DEPTH = 1

GRID_W = 64
HG_HEADS = 4
HG_DK = 128
HG_DV = 128
HG_W = HG_HEADS * HG_DK
HG_CHUNK = 64
AT_HEADS = 8
AT_KV_HEADS = 2
AT_GROUP = AT_HEADS // AT_KV_HEADS
AT_HD = 64
AT_W = AT_HEADS * AT_HD
AT_KV_W = AT_KV_HEADS * AT_HD
Q_BLOCK = 128
ROPE_THETA = 10000.0
ROPE_AXIS_DIM = AT_HD // 2
D_FF = 2816
CONV_W = 3
EPS = 1e-6

IN_SIZES = (HG_W, HG_W, HG_W, HG_W, HG_W,
            AT_W, AT_KV_W, AT_KV_W,
            D_MODEL, D_MODEL)
D_IN = int(sum(IN_SIZES))
IN_SPLITS = tuple(int(v) for v in np.cumsum(IN_SIZES)[:-1])

kernel_name = "hybrid_hgrn2_gqa2drope_convffn_encoder"


def rmsnorm(x, g):
    xf = x.astype(jnp.float32)
    return xf * lax.rsqrt(jnp.mean(xf * xf, axis=-1, keepdims=True) + EPS) * g.astype(jnp.float32)


def rotate_half(x):
    x1, x2 = jnp.split(x, 2, axis=-1)
    return jnp.concatenate([-x2, x1], axis=-1)


def axial_rope_tables(L):
    rows = L // GRID_W
    pos_row = jnp.broadcast_to(jnp.arange(rows, dtype=jnp.float32)[:, None], (rows, GRID_W)).reshape(L)
    pos_col = jnp.broadcast_to(jnp.arange(GRID_W, dtype=jnp.float32)[None, :], (rows, GRID_W)).reshape(L)
    inv = ROPE_THETA ** (-jnp.arange(0, ROPE_AXIS_DIM, 2, dtype=jnp.float32) / ROPE_AXIS_DIM)
    def tab(pos):
        ang = pos[:, None] * inv[None, :]
        ang = jnp.concatenate([ang, ang], axis=-1)
        return jnp.cos(ang)[None, :, None, :], jnp.sin(ang)[None, :, None, :]
    return tab(pos_row), tab(pos_col)


def apply_axial_rope(x, tabs):
    (cr, sr), (cc, sc) = tabs
    xr, xc = x[..., :ROPE_AXIS_DIM], x[..., ROPE_AXIS_DIM:]
    xr = xr * cr + rotate_half(xr) * sr
    xc = xc * cc + rotate_half(xc) * sc
    return jnp.concatenate([xr, xc], axis=-1)


def gla_chunkwise(q, k, v, g):
    B, H, L, dk = q.shape
    dv = v.shape[-1]
    C = HG_CHUNK
    N = L // C
    q = q.reshape(B, H, N, C, dk)
    k = k.reshape(B, H, N, C, dk)
    v = v.reshape(B, H, N, C, dv)
    b = jnp.cumsum(g.reshape(B, H, N, C, dk), axis=3)
    b_last = b[:, :, :, -1:, :]
    qd = q * jnp.exp(b)
    kd = k * jnp.exp(-b)
    A = jnp.einsum('bhncd,bhnsd->bhncs', qd, kd)
    mask = jnp.tril(jnp.ones((C, C), dtype=bool))
    A = jnp.where(mask, A, 0.0)
    intra = jnp.einsum('bhncs,bhnse->bhnce', A, v)
    kc = k * jnp.exp(b_last - b)
    U = jnp.einsum('bhnsd,bhnse->bhnde', kc, v)
    decay = jnp.exp(b_last[:, :, :, 0, :])

    def step(S, inp):
        dec, u = inp
        return dec[..., None] * S + u, S

    S0 = jnp.zeros((B, H, dk, dv), jnp.float32)
    _, S_prev = lax.scan(step, S0, (jnp.moveaxis(decay, 2, 0), jnp.moveaxis(U, 2, 0)))
    S_prev = jnp.moveaxis(S_prev, 0, 2)
    inter = jnp.einsum('bhncd,bhnde->bhnce', qd, S_prev)
    return (intra + inter).reshape(B, H, L, dv)


def hgrn2_bidirectional(hq, hi, hf_fwd, hf_bwd, hgate, lb_fwd, lb_bwd, onorm_g):
    B, L, _ = hq.shape
    def heads(t):
        return t.reshape(B, L, HG_HEADS, -1).transpose(0, 2, 1, 3)
    q = heads(jax.nn.silu(hq))
    v = heads(hi)

    def direction(fpre, lb):
        f = lb + (1.0 - lb) * jax.nn.sigmoid(fpre)
        return heads(1.0 - f), heads(jnp.log(f))

    k_f, g_f = direction(hf_fwd, lb_fwd)
    k_b, g_b = direction(hf_bwd, lb_bwd)
    o_f = gla_chunkwise(q, k_f, v, g_f)
    flip = lambda t: jnp.flip(t, axis=2)
    o_b = flip(gla_chunkwise(flip(q), flip(k_b), flip(v), flip(g_b)))
    o = (o_f + o_b).transpose(0, 2, 1, 3)
    o = rmsnorm(o, onorm_g) * jax.nn.silu(hgate.reshape(B, L, HG_HEADS, HG_DV))
    return o.reshape(B, L, HG_HEADS * HG_DV)


def gqa_bidirectional(aq, ak, av, q_norm_g, k_norm_g):
    B, L, _ = aq.shape
    tabs = axial_rope_tables(L)
    q = rmsnorm(aq.reshape(B, L, AT_HEADS, AT_HD), q_norm_g)
    k = rmsnorm(ak.reshape(B, L, AT_KV_HEADS, AT_HD), k_norm_g)
    v = av.reshape(B, L, AT_KV_HEADS, AT_HD).astype(jnp.float32)
    q = apply_axial_rope(q, tabs) * (AT_HD ** -0.5)
    k = apply_axial_rope(k, tabs)
    nb = L // Q_BLOCK
    qb = q.reshape(B, nb, Q_BLOCK, AT_KV_HEADS, AT_GROUP, AT_HD)
    qb = jnp.moveaxis(qb, 1, 0)

    def block(qblk):
        s = jnp.einsum('bqkgd,bskd->bkgqs', qblk, k)
        p = jax.nn.softmax(s, axis=-1)
        return jnp.einsum('bkgqs,bskd->bqkgd', p, v)

    o = lax.map(block, qb)
    o = jnp.moveaxis(o, 0, 1).reshape(B, L, AT_W)
    return o


def dwconv_centred(x, w, b):
    xp = jnp.pad(x, ((0, 0), (1, 1), (0, 0)))
    return xp[:, :-2] * w[0] + xp[:, 1:-1] * w[1] + xp[:, 2:] * w[2] + b


def setup_inputs(seed: int = 0) -> dict:
    key = jax.random.key(seed)
    ks = jax.random.split(key, 16)
    f32 = jnp.float32
    def nrm(k, shape, scale):
        return jax.random.normal(k, shape, f32) * scale
    return {
        "x": jax.random.normal(ks[0], (BATCH, SEQ, D_MODEL), f32),
        "norm1_g": 1.0 + nrm(ks[1], (DEPTH, D_MODEL), 0.02),
        "w_in": nrm(ks[2], (DEPTH, D_MODEL, D_IN), D_MODEL ** -0.5),
        "hg_lb_fwd": nrm(ks[3], (DEPTH + 1, HG_W), 0.1),
        "hg_lb_bwd": nrm(ks[4], (DEPTH + 1, HG_W), 0.1),
        "hg_onorm_g": 1.0 + nrm(ks[5], (DEPTH, HG_DV), 0.02),
        "q_norm_g": 1.0 + nrm(ks[6], (DEPTH, AT_HD), 0.02),
        "k_norm_g": 1.0 + nrm(ks[7], (DEPTH, AT_HD), 0.02),
        "w_branch_a": nrm(ks[8], (DEPTH, HG_HEADS * HG_DV, D_MODEL), (HG_HEADS * HG_DV) ** -0.5),
        "w_branch_b": nrm(ks[9], (DEPTH, AT_W, D_MODEL), AT_W ** -0.5),
        "w_out": nrm(ks[10], (DEPTH, D_MODEL, D_MODEL), 0.5 * D_MODEL ** -0.5),
        "norm2_g": 1.0 + nrm(ks[11], (DEPTH, D_MODEL), 0.02),
        "w_up": nrm(ks[12], (DEPTH, D_MODEL, 2 * D_FF), D_MODEL ** -0.5),
        "conv_w": nrm(ks[13], (DEPTH, CONV_W, 2 * D_FF), CONV_W ** -0.5),
        "conv_b": nrm(ks[14], (DEPTH, 2 * D_FF), 0.01),
        "w_down": nrm(ks[15], (DEPTH, D_FF, D_MODEL), 0.5 * D_FF ** -0.5),
    }


def reference(x, norm1_g, w_in, hg_lb_fwd, hg_lb_bwd, hg_onorm_g, q_norm_g, k_norm_g,
              w_branch_a, w_branch_b, w_out, norm2_g, w_up, conv_w, conv_b, w_down):
    dt = x.dtype
    lb_tab_f = jnp.cumsum(jax.nn.softmax(hg_lb_fwd.astype(jnp.float32), axis=0), axis=0)
    lb_tab_b = jnp.cumsum(jax.nn.softmax(hg_lb_bwd.astype(jnp.float32), axis=0), axis=0)
    for l in range(DEPTH):
        u = rmsnorm(x, norm1_g[l])
        proj = u @ w_in[l].astype(jnp.float32)
        hq, hi, hf_f, hf_b, hgate, aq, ak, av, ga, gb = jnp.split(proj, IN_SPLITS, axis=-1)
        o_a = hgrn2_bidirectional(hq, hi, hf_f, hf_b, hgate, lb_tab_f[l], lb_tab_b[l], hg_onorm_g[l])
        o_b = gqa_bidirectional(aq, ak, av, q_norm_g[l], k_norm_g[l])
        y_a = o_a @ w_branch_a[l].astype(jnp.float32)
        y_b = o_b @ w_branch_b[l].astype(jnp.float32)
        merged = jax.nn.sigmoid(ga) * y_a + jax.nn.sigmoid(gb) * y_b
        x = (x.astype(jnp.float32) + merged @ w_out[l].astype(jnp.float32)).astype(dt)
        hn = rmsnorm(x, norm2_g[l])
        up = hn @ w_up[l].astype(jnp.float32)
        up = dwconv_centred(up, conv_w[l].astype(jnp.float32), conv_b[l].astype(jnp.float32))
        val, gate = jnp.split(up, 2, axis=-1)
        ff = (jax.nn.silu(gate) * val) @ w_down[l].astype(jnp.float32)
        x = (x.astype(jnp.float32) + ff).astype(dt)
    return x
```

```python
import contextlib
import numpy as np
import concourse.bass as bass
import concourse.mybir as mybir
from concourse.bass_utils import run_bass_kernel_spmd

F32 = mybir.dt.float32
BF16 = mybir.dt.bfloat16
ALU = mybir.AluOpType
AF = mybir.ActivationFunctionType
AX = mybir.AxisListType

import os
SKIP = os.environ.get("SKIP", "")
D = 1024
KC = 8
DFF = 2816
EPS = 1e-6
C_HQ, C_HI, C_FF, C_FB, C_HG, C_AQ, C_AK, C_AV, C_GA, C_GB = 0, 512, 1024, 1536, 2048, 2560, 3072, 3200, 3328, 4352


class Res:
    __slots__ = ("w", "r")

    def __init__(self):
        self.w = {}
        self.r = {}


class Buf:
    def __init__(self, t):
        self.t = t
        self.res = Res()

    def __getitem__(self, idx):
        return self.t[idx]


def _res(x):
    return x.res if isinstance(x, Buf) else x


class Prog:
    ENGS = ("pe", "act", "dve", "pool", "sp")

    def __init__(self, nc):
        self.nc = nc
        self.ops = {e: [] for e in self.ENGS}
        self.dma_sems = {}
        self.pending = {}

    def _collect(self, eng, reads, writes):
        deps = list(self.pending.pop(eng, []))
        for r in reads:
            deps.extend(_res(r).w.values())
        for w in writes:
            w = _res(w)
            for d in list(w.w.values()) + list(w.r.values()):
                if d[0] == "e" and d[1] == eng and (eng == "pe" or os.environ.get("NOSELF")):
                    continue
                deps.append(d)
        return deps

    def op(self, eng, thunk, reads=(), writes=()):
        deps = self._collect(eng, reads, writes)
        idx = len(self.ops[eng])
        self.ops[eng].append([deps, thunk, None])
        if os.environ.get("DUMP"):
            import sys as _s
            f = _s._getframe(1); ln = []
            while f is not None and len(ln) < 3:
                ln.append(f.f_lineno); f = f.f_back
            self.ops[eng][-1].append(ln)
        me = ("e", eng, idx)
        for r in reads:
            _res(r).r[("e", eng)] = me
        for w in writes:
            w = _res(w)
            w.w = {("e", eng): me}
            w.r = {}
        return me

    def dma(self, queue, thunk, semkey, reads=(), writes=()):
        deps = self._collect(queue, reads, writes)
        c = self.dma_sems.setdefault(semkey, [0])
        c[0] += 16
        self.ops[queue].append([deps, thunk, semkey])
        if os.environ.get("DUMP"):
            import sys as _s
            f = _s._getframe(1); ln = []
            while f is not None and len(ln) < 3:
                ln.append(f.f_lineno); f = f.f_back
            self.ops[queue][-1].append(ln)
        me = ("d", semkey, c[0])
        for r in reads:
            _res(r).r[("d", semkey)] = me
        for w in writes:
            w = _res(w)
            w.w = {("d", semkey): me}
            w.r = {}
        return me

    def barrier(self):
        deps = []
        for e in self.ENGS:
            for i in range(len(self.ops[e]) - 1, -1, -1):
                if self.ops[e][i][2] is None:
                    deps.append(("e", e, i))
                    break
        for k, c in self.dma_sems.items():
            deps.append(("d", k, c[0]))
        for e in self.ENGS:
            self.pending[e] = list(self.pending.get(e, [])) + deps

    def build(self, final_deps=()):
        nc = self.nc
        final_deps = list(final_deps) + [("d", k, c[0]) for k, c in self.dma_sems.items()]
        needed = {e: set() for e in self.ENGS}
        for e in self.ENGS:
            for idx, rec in enumerate(self.ops[e]):
                for d in rec[0]:
                    if d[0] == "e" and not (d[1] == e and d[2] >= idx):
                        needed[d[1]].add(d[2])
        for d in final_deps:
            if d[0] == "e":
                needed[d[1]].add(d[2])
        semval = {}
        for e in self.ENGS:
            for rank, idx in enumerate(sorted(needed[e])):
                semval[(e, idx)] = rank + 1
        with contextlib.ExitStack() as st:
            esem = {e: st.enter_context(nc.semaphore("s_" + e)) for e in self.ENGS}
            print("n dma sems", len(self.dma_sems), {e: len(self.ops[e]) for e in self.ENGS})
            dsem = {k: st.enter_context(nc.semaphore("d_%d" % i)) for i, k in enumerate(self.dma_sems)}
            block = st.enter_context(nc.Block())

            def replay(e, h):
                waited = {}
                for idx, rec in enumerate(self.ops[e]):
                    deps, thunk, dk = rec[0], rec[1], rec[2]
                    dump = []
                    for d in deps:
                        if d[0] == "e":
                            if d[1] == e and d[2] >= idx:
                                continue
                            key = ("e", d[1]); sem = esem[d[1]]; val = semval[(d[1], d[2])]
                        else:
                            key = ("d", d[1]); sem = dsem[d[1]]; val = d[2]
                        if waited.get(key, 0) >= val:
                            continue
                        waited[key] = val
                        h.wait_ge(sem, val)
                        dump.append((key[1], val))
                    if os.environ.get("DUMP"):
                        print("OP", e, idx, "lines", rec[3], "waits", dump, "dma" if dk else "", "inc", semval.get((e, idx)))
                    ins = thunk(h)
                    if dk is not None:
                        ins.then_inc(dsem[dk], 16)
                    elif (e, idx) in semval:
                        ins.then_inc(esem[e], 1)
                if e == "sp":
                    for d in final_deps:
                        if d[0] == "e":
                            h.wait_ge(esem[d[1]], semval[(d[1], d[2])])
                        else:
                            h.wait_ge(dsem[d[1]], d[2])

            @block.tensor
            def _(h):
                replay("pe", h)

            @block.scalar
            def _(h):
                replay("act", h)

            @block.vector
            def _(h):
                replay("dve", h)

            @block.gpsimd
            def _(h):
                replay("pool", h)

            @block.sync
            def _(h):
                replay("sp", h)


class Ring:
    def __init__(self, bufs):
        self.bufs = bufs
        self.i = 0

    def next(self):
        b = self.bufs[self.i % len(self.bufs)]
        self.i += 1
        return b


class _Stop(Exception):
    pass


def build_program(SEQ, debug=False, stop=None):
    NA = SEQ // 128
    OWN = SEQ // 4
    NE = OWN // 128 + 1
    NEXT = NE * 128
    nc = bass.Bass("TRN2", target_bir_lowering=False)

    def din(name, shape, dt=F32):
        return nc.dram_tensor(name, list(shape), dt, kind="ExternalInput").ap()

    xall = din("xall", [SEQ, D]); xext = din("xext", [NEXT, D])
    cosk = din("cosk", [SEQ, 128]); sink = din("sink", [SEQ, 128])
    cosq = din("cosq", [NEXT, 512]); sinq = din("sinq", [NEXT, 512])
    mpre = din("mpre", [128, NA]); mpost = din("mpost", [128, NA]); cval = din("cval", [128, 2])
    tri = din("tri", [128, 6, 128]); cind = din("cind", [128, 2]); ident = din("ident", [128, 128])
    w_in = din("w_in", [D, 5376]); w_a = din("w_a", [512, D]); w_b = din("w_b", [512, D])
    w_out = din("w_out", [D, D]); w_up = din("w_up", [44, 128, KC, 128]); w_down = din("w_down", [DFF, D])
    g1 = din("g1", [1, D]); g2 = din("g2", [1, D]); lbf = din("lbf", [2, 512]); lbb = din("lbb", [2, 512])
    onorm = din("onorm", [1, 512]); qg = din("qg", [1, 512]); kg = din("kg", [1, 128])
    cw = din("cw", [128, 44, 3]); cb = din("cb", [128, 44])
    y = nc.dram_tensor("y", [OWN, D], F32, kind="ExternalOutput").ap()
    skind = "ExternalOutput" if debug else "Internal"
    PROJ = nc.dram_tensor("PROJ", [NEXT, 5120], F32, kind=skind).ap()
    OBWD = nc.dram_tensor("OBWD", [NEXT, 512], F32, kind=skind).ap()
    HNT = nc.dram_tensor("HNT", [D, NEXT], BF16, kind=skind).ap()
    if debug:
        DBG_OB = nc.dram_tensor("DBG_OB", [128, 4, NEXT], BF16, kind="ExternalOutput").ap()
        DBG_OA = nc.dram_tensor("DBG_OA", [NEXT, 512], F32, kind="ExternalOutput").ap()
        DBG_S = nc.dram_tensor("DBG_S", [128, 2, 512], F32, kind="ExternalOutput").ap()
        DBG_H = nc.dram_tensor("DBG_H", [NEXT, D], F32, kind="ExternalOutput").ap()
    r_PROJ = [Res() for _ in range(NE)]; r_OBWD = [Res() for _ in range(NE)]; r_HNT = [Res() for _ in range(NE)]
    r_y = [Res() for _ in range(OWN // 128)]; r_dbg = Res()

    P = Prog(nc)
    out_deps = []
    uid = [0]

    def nm(p):
        uid[0] += 1
        return "%s%d" % (p, uid[0])

    with contextlib.ExitStack() as S0:
        sb_live = [0, 0]

        def SB(st, shape, dt, name="b"):
            nbytes = int(np.prod(shape[1:])) * (2 if dt == BF16 else 4)
            nbytes = (nbytes + 31) // 32 * 32

            def _free(nbytes=nbytes):
                sb_live[0] -= nbytes
            st.callback(_free)
            sb_live[0] += nbytes
            if sb_live[0] > sb_live[1]:
                sb_live[1] = sb_live[0]
                if os.environ.get("SBDBG"):
                    print("SBUF high-water", sb_live[1], "at", name)
            return Buf(st.enter_context(nc.sbuf_tensor(nm(name), list(shape), dt)))

        def SBring(st, n, shape, dt, name="r"):
            return Ring([SB(st, shape, dt, name) for _ in range(n)])

        psF = Ring([Buf(S0.enter_context(nc.psum_tensor(nm("pf"), [128, 512], F32))) for _ in range(6)])
        psB = Ring([Buf(S0.enter_context(nc.psum_tensor(nm("pb"), [128, 8, 128], BF16))) for _ in range(2)])

        def act(out, in_, func, rd, wr, **kw):
            P.op("act", lambda h: h.activation(out=out, in_=in_, func=func, **kw), rd, wr)

        def tt(eng, out, in0, in1, op, rd, wr):
            P.op(eng, lambda h: h.tensor_tensor(out=out, in0=in0, in1=in1, op=op), rd, wr)

        def ts(eng, out, in0, s1, s2, op0, op1, rd, wr):
            if op1 is None:
                P.op(eng, lambda h: h.tensor_scalar(out=out, in0=in0, scalar1=s1, scalar2=None, op0=op0), rd, wr)
            else:
                P.op(eng, lambda h: h.tensor_scalar(out=out, in0=in0, scalar1=s1, scalar2=s2, op0=op0, op1=op1), rd, wr)

        def stt(out, in0, scalar, in1, op0, op1, rd, wr):
            P.op("dve", lambda h: h.scalar_tensor_tensor(out=out, in0=in0, scalar=scalar, in1=in1, op0=op0, op1=op1), rd, wr)

        def cp(eng, out, in_, rd, wr):
            if eng == "act":
                P.op("act", lambda h: h.copy(out=out, in_=in_), rd, wr)
            else:
                P.op(eng, lambda h: h.tensor_copy(out=out, in_=in_), rd, wr)

        def act_sigmoid(out_ap, in_ap, rd_in, buf, neg):
            act(out_ap, in_ap, AF.Exp, rd_in, [buf], scale=(1.0 if neg else -1.0))
            act(out_ap, out_ap, AF.Ln, [buf], [buf], bias=1.0)
            act(out_ap, out_ap, AF.Exp, [buf], [buf], scale=-1.0)

        def mm(out, lhsT, rhs, rd, wr, start=True, stop=True):
            P.op("pe", lambda h: h.matmul(out, lhsT=lhsT, rhs=rhs, start=start, stop=stop), rd, wr)

        def tr(out, in_, idn, rd, wr):
            P.op("pe", lambda h: h.transpose(out=out, in_=in_, identity=idn), rd, wr)

        dq = [0]

        setup = [True]
        r_wl = Res()

        def ld(out, in_, wr, rd=(), key=None, q="sp"):
            return P.dma(q, lambda h: h.dma_start(out=out, in_=in_), key or ("setupL" if setup[0] else nm("L")), rd, wr)

        def ldw(out, in_, wr, key=None):
            if key is None:
                return P.dma("pool", lambda h: h.dma_start(out=out, in_=in_), "wload", (), list(wr) + [r_wl])
            return P.dma("pool", lambda h: h.dma_start(out=out, in_=in_), key, (), wr)

        identb = SB(S0, [128, 128], BF16); ldw(identb[:], ident, [identb])
        trif = SB(S0, [128, 6, 128], F32); ld(trif[:], tri, [trif])
        T_INCF, T_INCB, T_SUFF64, T_SUFB64, T_SUFF128, T_SUFB128 = range(6)
        cindf = SB(S0, [128, 2], F32); ld(cindf[:], cind, [cindf])
        onesf = SB(S0, [128, 128], F32)
        P.op("dve", lambda h: h.memset(onesf[:], 1.0), (), [onesf])
        mask4 = {}
        for nmk, ti in (("f", T_INCF), ("b", T_INCB)):
            m4 = SB(S0, [128, 4, 128], F32)
            for hh in range(4):
                ld(m4[:, hh, :], tri[:, ti, :], [m4])
            mask4[nmk] = m4
        g1b = SB(S0, [128, D], F32); ld(g1b[:], g1.broadcast_to([128, D]), [g1b])
        g2b = SB(S0, [128, D], F32); ld(g2b[:], g2.broadcast_to([128, D]), [g2b])
        onb = SB(S0, [128, 512], F32); ld(onb[:], onorm.broadcast_to([128, 512]), [onb])
        qgb = SB(S0, [128, 512], F32); ld(qgb[:], qg.broadcast_to([128, 512]), [qgb])
        kgb = SB(S0, [128, 128], F32); ld(kgb[:], kg.broadcast_to([128, 128]), [kgb])
        mpr = SB(S0, [128, NA], F32); ld(mpr[:], mpre, [mpr])
        mpo = SB(S0, [128, NA], F32); ld(mpo[:], mpost, [mpo])
        cvl = SB(S0, [128, 2], F32); ld(cvl[:], cval, [cvl])
        cwt = SB(S0, [128, 44, 3], F32); ld(cwt[:], cw, [cwt])
        cbt = SB(S0, [128, 44], F32); ld(cbt[:], cb, [cbt])
        oml = {"f": SB(S0, [128, 512], F32), "b": SB(S0, [128, 512], F32)}
        with contextlib.ExitStack() as St:
            for nmk, lb in (("f", lbf), ("b", lbb)):
                a0 = SB(St, [128, 512], F32); a1 = SB(St, [128, 512], F32); om = oml[nmk]
                ld(a0[:], lb[0:1, :].broadcast_to([128, 512]), [a0], key="a0" + nmk)
                ld(a1[:], lb[1:2, :].broadcast_to([128, 512]), [a1], key="a1" + nmk)
                tt("dve", a1[:], a1[:], a0[:], ALU.subtract, [a0, a1], [a1])
                act(om[:], a1[:], AF.Sigmoid, [a1], [om])
            P.barrier()
        setup[0] = False
        obT = SB(S0, [128, 4, NEXT], BF16)
        Sin = {"f": SB(S0, [128, 4, 128], F32), "b": SB(S0, [128, 4, 128], F32)}
        Pd = SB(S0, [128, 4], F32)
        for k_ in ("f", "b"):
            P.op("dve", lambda h, k_=k_: h.memset(Sin[k_][:], 0.0), (), [Sin[k_]])
        P.op("dve", lambda h: h.memset(Pd[:], 1.0), (), [Pd])

        def rms_part1(st_bufs, xt, gb):
            ssr, rsr, xnr, uTr = st_bufs
            ss = ssr.next(); rs = rsr.next(); xn = xnr.next()
            P.op("act", lambda h: h.activation(out=xn[:], in_=xt[:], func=AF.Square, accum_out=ss[:]), [xt], [xn, ss])
            act(rs[:], ss[:], AF.Ln, [ss], [rs], scale=1.0 / D, bias=EPS)
            act(rs[:], rs[:], AF.Exp, [rs], [rs], scale=-0.5)
            stt(xn[:], xt[:], rs[:, 0:1], gb[:], ALU.mult, ALU.mult, [xt, rs, gb], [xn])
            return xn

        def rms_part2(st_bufs, xn):
            uT = st_bufs[3].next()
            pT = psB.next()
            for c in range(KC):
                tr(pT[:, c, :], xn[:, c * 128:(c + 1) * 128], identb[:], [xn, identb], [pT])
            cp("act", uT[:], pT[:], [pT], [uT])
            return uT

        def rmsnorm_T(st_bufs, xt, gb):
            return rms_part2(st_bufs, rms_part1(st_bufs, xt, gb))

        def norm_bufs(st):
            return (SBring(st, 2, [128, 1], F32), SBring(st, 2, [128, 1], F32),
                    SBring(st, 2, [128, D], BF16), SBring(st, 2, [128, KC, 128], BF16))

        def proj_tok(ps, uT, w, c0, ncols):
            for c in range(KC):
                mm(ps[:, 0:ncols], uT[:, c, :], w[:, c, c0:c0 + ncols], [uT, w], [ps], start=(c == 0), stop=(c == KC - 1))

        def head_rms(st_r, src, nh, hd, gbt, out):
            HRM = os.environ.get("HRM", "")
            sq, ssr, rsr = st_r
            q2 = sq.next(); s2 = ssr.next(); r2 = rsr.next()
            tt("dve" if HRM == "nopool" else "pool", q2[:, 0:nh * hd], src[:, 0:nh * hd], src[:, 0:nh * hd], ALU.mult, [src], [q2])
            if HRM == "onlysq":
                return
            P.op("dve", lambda h: h.tensor_reduce(out=s2[:, 0:nh], in_=q2[:, 0:nh * hd].rearrange("p (a b) -> p a b", b=hd),
                                                  axis=AX.X, op=ALU.add), [q2], [s2])
            if HRM == "onlyred":
                return
            act(r2[:, 0:nh], s2[:, 0:nh], AF.Ln, [s2], [r2], scale=1.0 / hd, bias=EPS)
            act(r2[:, 0:nh], r2[:, 0:nh], AF.Exp, [r2], [r2], scale=-0.5)
            if HRM == "nostt":
                return
            for hh in range(nh):
                stt(out[:, hh * hd:(hh + 1) * hd], src[:, hh * hd:(hh + 1) * hd], r2[:, hh:hh + 1],
                    gbt[:, hh * hd:(hh + 1) * hd], ALU.mult, ALU.mult, [src, r2, gbt], [out])

        def rope_parts(st_r, xin, W, ct, sn_):
            t1 = st_r[0].next(); t2 = st_r[1].next()
            tt("dve", t1[:, 0:W], xin[:, 0:W], ct[:, 0:W], ALU.mult, [xin, ct], [t1])
            x4 = xin[:, 0:W].rearrange("p (a b c) -> p a b c", b=2, c=16)
            s4 = sn_[:, 0:W].rearrange("p (a b c) -> p a b c", b=2, c=16)
            o4 = t2[:, 0:W].rearrange("p (a b c) -> p a b c", b=2, c=16)
            tt("pool", o4[:, :, 0, :], x4[:, :, 1, :], s4[:, :, 0, :], ALU.mult, [xin, sn_], [t2])
            tt("pool", o4[:, :, 1, :], x4[:, :, 0, :], s4[:, :, 1, :], ALU.mult, [xin, sn_], [t2])
            return t1, t2

        open_stacks = []
        try:
            wv_ = w_in.rearrange("(c p) n -> p c n", p=128)
            SQ = contextlib.ExitStack(); open_stacks.append(SQ)
            QT = SB(SQ, [128, 8, NEXT], BF16)
            P.op("pool", lambda h: h.memset(QT[:].rearrange("p a b -> p (a b)"), 0.0), (), [QT])
            if os.environ.get("SBDBG"):
                print("live after QT", sb_live[0])
            for pas in range(2):
              with contextlib.ExitStack() as S2:
                ng = 6 if pas == 0 else 4
                wP = SB(S2, [128, KC, ng * 512], BF16)
                for gi in range(ng):
                    src0 = gi * 512 if pas == 0 else C_GA + gi * 512
                    ldw(wP[:, :, gi * 512:(gi + 1) * 512], wv_[:, :, src0:src0 + 512], [wP])
                nb = norm_bufs(S2)
                xr = SBring(S2, 2, [128, D], F32, "xe")
                cqr = SBring(S2, 2, [128, 512], F32); sqr = SBring(S2, 2, [128, 512], F32)
                evr = SBring(S2, 4, [128, 512], F32)
                qn_r = SBring(S2, 1, [128, 512], F32); qrb_r = SBring(S2, 2, [128, 512], BF16)
                hr = (SBring(S2, 1, [128, 512], F32), SBring(S2, 2, [128, 8], F32), SBring(S2, 2, [128, 8], F32))
                rr = (SBring(S2, 1, [128, 512], F32), SBring(S2, 1, [128, 512], F32))
                def p2_front(t):
                    xe = xr.next()
                    ld(xe[:], xext[t * 128:(t + 1) * 128, :], [xe], key="xe%d" % (t % 2))
                    return rms_part1(nb, xe, g1b)

                xn_next = p2_front(0)
                for t in range(NE):
                    if pas == 0:
                        cq = cqr.next(); sq_ = sqr.next()
                        ld(cq[:], cosq[t * 128:(t + 1) * 128, :], [cq], key="cq%d" % (t % 2))
                        ld(sq_[:], sinq[t * 128:(t + 1) * 128, :], [sq_], key="sq%d" % (t % 2))
                    uT = rms_part2(nb, xn_next)
                    if t + 1 < NE:
                        xn_next = p2_front(t + 1)
                    for gi in range(ng):
                        ps = psF.next(); proj_tok(ps, uT, wP, gi * 512, 512)
                        evk = evr.i % 4
                        ev = evr.next()
                        cp("act" if gi % 2 == 0 else "dve", ev[:], ps[:], [ps], [ev])
                        gcol = gi * 512 if pas == 0 else 3072 + gi * 512
                        ld(PROJ[t * 128:(t + 1) * 128, gcol:gcol + 512], ev[:], [r_PROJ[t]], [ev], key="pj%d" % evk)
                        if pas == 0 and gi == 5:
                            qn = qn_r.next(); qrb = qrb_r.next()
                            head_rms(hr, ev, 8, 64, qgb, qn)
                            t1, t2 = rope_parts(rr, qn, 512, cq, sq_)
                            tt("dve", qrb[:].rearrange("p (j k d) -> p k j d", k=2, d=64),
                               t1[:].rearrange("p (k j d) -> p k j d", k=2, d=64),
                               t2[:].rearrange("p (k j d) -> p k j d", k=2, d=64), ALU.add, [t1, t2], [qrb])
                            pT = psB.next()
                            for jj in range(4):
                                tr(pT[:, jj, :], qrb[:, jj * 128:(jj + 1) * 128], identb[:], [qrb, identb], [pT])
                            cp("act", QT[0:64, 0:4, t * 128:(t + 1) * 128], pT[0:64, 0:4, :], [pT], [QT])
                            cp("act", QT[64:128, 4:8, t * 128:(t + 1) * 128], pT[64:128, 0:4, :], [pT], [QT])
                P.barrier()
            if stop == "P2":
                raise _Stop()

            SKV = contextlib.ExitStack(); open_stacks.append(SKV)
            KT = SB(SKV, [128, SEQ], BF16)
            V1 = SB(SKV, [128, NA, 2, 128], BF16)
            P.op("pool", lambda h: h.memset(V1[:].rearrange("p a b c -> p (a b c)"), 0.0), (), [V1])
            P.op("pool", lambda h: h.memset(V1[:, :, 0, 64:65], 1.0), (), [V1])
            P.op("pool", lambda h: h.memset(V1[:, :, 1, 0:1], 1.0), (), [V1])
            with contextlib.ExitStack() as S1:
                wA = SB(S1, [128, KC, 1792], BF16)
                ldw(wA[:, :, 0:512], wv_[:, :, C_HI:C_HI + 512], [wA])
                ldw(wA[:, :, 512:1024], wv_[:, :, C_FF:C_FF + 512], [wA])
                ldw(wA[:, :, 1024:1536], wv_[:, :, C_FB:C_FB + 512], [wA])
                ldw(wA[:, :, 1536:1792], wv_[:, :, C_AK:C_AK + 256], [wA])
                nb = norm_bufs(S1)
                if os.environ.get("SBDBG"):
                    print("live in P1", sb_live[0])
                xr = SBring(S1, 2, [128, D], F32, "xa")
                ckr = SBring(S1, 2, [128, 128], F32); skr = SBring(S1, 2, [128, 128], F32)
                vr = SBring(S1, 2, [128, 512], BF16)
                kr_ = SBring(S1, 2, [128, 512], F32); gr = SBring(S1, 2, [128, 512], F32)
                Er = SBring(S1, 2, [128, 512], F32); kcr = SBring(S1, 2, [128, 512], BF16); decr = SBring(S1, 2, [128, 8], F32)
                kraw_r = SBring(S1, 2, [128, 128], F32); kn_r = SBring(S1, 2, [128, 128], F32); krb_r = SBring(S1, 2, [128, 128], BF16)
                hr = (SBring(S1, 1, [128, 128], F32), SBring(S1, 2, [128, 8], F32), SBring(S1, 2, [128, 8], F32))
                rr = (SBring(S1, 1, [128, 128], F32), SBring(S1, 1, [128, 128], F32))
                def p1_front(j):
                    xa = xr.next()
                    ld(xa[:], xall[j * 128:(j + 1) * 128, :], [xa], key="xa%d" % (j % 2))
                    return rms_part1(nb, xa, g1b)

                snfr = SBring(S1, 2, [128, 512], F32, "snf"); snbr = SBring(S1, 2, [128, 512], F32, "snb")
                stB = {}

                def b_ew(dr):
                    jb, sn_b = stB["j"], stB["sn" + dr]
                    mk = mpr if dr == "f" else mpo
                    k = kr_.next(); g = gr.next()
                    stt(k[:], sn_b[:], mk[:, jb:jb + 1], oml[dr][:], ALU.mult, ALU.mult, [sn_b, mk, oml[dr]], [k])
                    act(g[:], k[:], AF.Ln, [k], [g], scale=-1.0, bias=1.0)
                    stB["k" + dr], stB["g" + dr] = k, g

                def b_suf(dr):
                    k, g = stB["k" + dr], stB["g" + dr]
                    ps_s = psF.next()
                    ti = T_SUFF128 if dr == "f" else T_SUFB128
                    mm(ps_s[:], trif[:, ti, :], g[:], [trif, g], [ps_s])
                    E = Er.next(); kc = kcr.next()
                    act(E[:], ps_s[:], AF.Exp, [ps_s], [E])
                    tt("dve", kc[:], k[:], E[:], ALU.mult, [k, E], [kc])
                    stB["kc" + dr] = kc

                def b_p2(dr):
                    v_b, g, kc = stB["v"], stB["g" + dr], stB["kc" + dr]
                    dec = decr.next()
                    ps_d = psF.next()
                    for hh in range(4):
                        mm(ps_d[:, 2 * hh:2 * hh + 2], g[:, hh * 128:(hh + 1) * 128], onesf[:, 0:2], [g, onesf], [ps_d])
                    act(dec[:], ps_d[:, 0:8], AF.Exp, [ps_d], [dec])
                    ps_U = psF.next()
                    for hh in range(4):
                        mm(ps_U[:, hh * 128:(hh + 1) * 128], kc[:, hh * 128:(hh + 1) * 128], v_b[:, hh * 128:(hh + 1) * 128], [kc, v_b], [ps_U])
                    Sd = Sin[dr]
                    for hh in range(4):
                        if dr == "f":
                            stt(Sd[:, hh, :], Sd[:, hh, :], dec[:, 2 * hh:2 * hh + 1], ps_U[:, hh * 128:(hh + 1) * 128],
                                ALU.mult, ALU.add, [Sd, dec, ps_U], [Sd])
                        else:
                            stt(Sd[:, hh, :], ps_U[:, hh * 128:(hh + 1) * 128], Pd[:, hh:hh + 1], Sd[:, hh, :],
                                ALU.mult, ALU.add, [Sd, Pd, ps_U], [Sd])
                    if dr == "b":
                        tt("dve", Pd[:], Pd[:], dec[:, 0:8:2], ALU.mult, [Pd, dec], [Pd])

                xn_next = p1_front(0)
                for j in range(NA + 1):
                    haveA = j < NA
                    haveB = j >= 1
                    if haveB:
                        b_ew("f"); b_ew("b")
                    if haveA:
                        ck = ckr.next(); sk = skr.next()
                        ld(ck[:], cosk[j * 128:(j + 1) * 128, :], [ck], key="ck%d" % (j % 2))
                        ld(sk[:], sink[j * 128:(j + 1) * 128, :], [sk], key="sk%d" % (j % 2))
                        uT = rms_part2(nb, xn_next)
                        if j + 1 < NA:
                            xn_next = p1_front(j + 1)
                        ps_v = psF.next(); proj_tok(ps_v, uT, wA, 0, 512)
                        v = vr.next(); cp("dve", v[:], ps_v[:], [ps_v], [v])
                    if haveB:
                        b_suf("f"); b_suf("b")
                    if haveA:
                        ps_ff = psF.next(); proj_tok(ps_ff, uT, wA, 512, 512)
                        snf = snfr.next()
                        act_sigmoid(snf[:], ps_ff[:], [ps_ff], snf, True)
                    if haveB:
                        b_p2("f")
                    if haveA:
                        ps_fb = psF.next(); proj_tok(ps_fb, uT, wA, 1024, 512)
                        snb = snbr.next()
                        act_sigmoid(snb[:], ps_fb[:], [ps_fb], snb, True)
                    if haveB:
                        b_p2("b")
                    if haveA:
                        ps_kv = psF.next(); proj_tok(ps_kv, uT, wA, 1536, 256)
                        kraw = kraw_r.next(); kn = kn_r.next(); krb = krb_r.next()
                        cp("act", kraw[:], ps_kv[:, 0:128], [ps_kv], [kraw])
                        P.op("act", lambda h, j=j, ps_kv=ps_kv: h.copy(out=V1[:, j, 0, 0:64], in_=ps_kv[:, 128:192]), [ps_kv], [V1])
                        P.op("act", lambda h, j=j, ps_kv=ps_kv: h.copy(out=V1[:, j, 1, 64:128], in_=ps_kv[:, 192:256]), [ps_kv], [V1])
                        head_rms(hr, kraw, 2, 64, kgb, kn)
                        t1, t2 = rope_parts(rr, kn, 128, ck, sk)
                        tt("dve", krb[:], t1[:], t2[:], ALU.add, [t1, t2], [krb])
                        pT = psB.next()
                        tr(pT[:, 0, :], krb[:], identb[:], [krb, identb], [pT])
                        cp("act", KT[:, j * 128:(j + 1) * 128], pT[:, 0, :], [pT], [KT])
                    if haveA:
                        stB = {"j": j, "v": v, "snf": snf, "snb": snb}
                P.barrier()

            if debug:
                out_deps.append(ld(DBG_S[:, 0, :], Sin["f"][:].rearrange("p a b -> p (a b)"), [r_dbg], [Sin["f"]]))
                out_deps.append(ld(DBG_S[:, 1, :], Sin["b"][:].rearrange("p a b -> p (a b)"), [r_dbg], [Sin["b"]]))

            if stop == "P1":
                raise _Stop()
            with contextlib.ExitStack() as S3:
                pr = SBring(S3, 4, [128, 512], BF16, "p")
                ocr = SBring(S3, 2, [128, 512], F32); rrow = SBring(S3, 2, [128, 512], F32)
                psS = Ring(psF.bufs[0:3]); psO = Ring(psF.bufs[3:5]); psX = Ring(psF.bufs[5:6])
                qblocks = [(q0, min(512, NEXT - q0)) for q0 in range(0, NEXT, 512)]
                items = [(hd_, q0, nq, kt) for hd_ in range(8) for (q0, nq) in qblocks for kt in range(NA)]
                PF = 2

                def issue_s(it):
                    hd_, q0, nq, kt = it
                    kv, jj = hd_ // 4, hd_ % 4
                    pl = slice(kv * 64, (kv + 1) * 64)
                    ps_s = psS.next()
                    mm(ps_s[:, 0:nq], KT[:, kt * 128:(kt + 1) * 128], QT[:, hd_, q0:q0 + nq], [KT, QT], [ps_s])
                    return ps_s

                pend = [issue_s(items[i]) for i in range(min(PF, len(items)))]
                ps_o = None
                for i, (hd_, q0, nq, kt) in enumerate(items):
                    kv, jj = hd_ // 4, hd_ % 4
                    pl = slice(kv * 64, (kv + 1) * 64)
                    dr_ = 64 if kv == 0 else 0
                    ps_s = pend.pop(0)
                    if i + PF < len(items):
                        pend.append(issue_s(items[i + PF]))
                    if kt == 0:
                        ps_o = psO.next()
                    p = pr.next()
                    act(p[:, 0:nq], ps_s[:, 0:nq], AF.Exp, [ps_s], [p], scale=0.125)
                    mm(ps_o[:, 0:nq], V1[:, kt, kv, :], p[:, 0:nq], [V1, p], [ps_o], start=(kt == 0), stop=(kt == NA - 1))
                    if kt == NA - 1:
                        oc = ocr.next(); rw = rrow.next()
                        cp("act", oc[:, 0:nq], ps_o[:, 0:nq], [ps_o], [oc])
                        act(rw[dr_:dr_ + 1, 0:nq], oc[dr_:dr_ + 1, 0:nq], AF.Ln, [oc], [rw])
                        act(rw[dr_:dr_ + 1, 0:nq], rw[dr_:dr_ + 1, 0:nq], AF.Exp, [rw], [rw], scale=-1.0)
                        ps_b = psX.next()
                        mm(ps_b[:, 0:nq], onesf[dr_:dr_ + 1, :], rw[dr_:dr_ + 1, 0:nq], [onesf, rw], [ps_b])
                        tt("dve", obT[pl, jj, q0:q0 + nq], oc[pl, 0:nq], ps_b[pl, 0:nq], ALU.mult, [oc, ps_b], [obT])
                P.barrier()
            SKV.close()
            SQ.close()
            if debug:
                out_deps.append(ld(DBG_OB, obT[:], [r_dbg], [obT]))

            def hgrn_sweep(dr, st, merge):
                PJ = SBring(st, 2, [128, 512], F32, "hq"); PI = SBring(st, 2, [128, 512], F32, "hi")
                PFr = SBring(st, 2, [128, 512], F32, "hf")
                qsr = SBring(st, 1, [128, 512], BF16); snr = SBring(st, 1, [128, 512], F32); kr_ = SBring(st, 1, [128, 512], F32)
                kbr = SBring(st, 1, [128, 512], BF16); gr = SBring(st, 1, [128, 512], F32); vr = SBring(st, 2, [128, 512], BF16)
                E1r = SBring(st, 1, [128, 4, 128], F32); E2r = SBring(st, 1, [128, 4, 128], F32); E3r = SBring(st, 1, [128, 512], F32)
                der = SBring(st, 2, [128, 4, 2], F32)
                qdr = SBring(st, 2, [128, 4, 128], BF16); kdr = SBring(st, 1, [128, 4, 128], BF16); kcr = SBring(st, 2, [128, 512], BF16)
                atr = SBring(st, 1, [128, 4, 128], BF16); oir = SBring(st, 2, [128, 512], F32); oor = SBring(st, 2, [128, 512], F32)
                S32r = SBring(st, 3, [128, 4, 128], F32); Sbfr = SBring(st, 3, [128, 4, 128], BF16)
                qsgr = SBring(st, 1, [128, 512], F32)
                ti_inc = T_INCF if dr == "f" else T_INCB
                ti_suf = T_SUFF64 if dr == "f" else T_SUFB64
                c_hf = 1024 if dr == "f" else 1536
                order = [0, 1] if dr == "f" else [1, 0]
                rows = [slice(0, 64), slice(64, 128)]
                if merge:
                    obr = SBring(st, 2, [128, 512], F32, "ob"); PGr = SBring(st, 2, [128, 512], F32, "hg")
                    GAr = SBring(st, 2, [128, 512], F32, "ga"); GBr = SBring(st, 2, [128, 512], F32, "gb")
                    xer = SBring(st, 1, [128, D], F32, "xe2")
                    hr = (SBring(st, 1, [128, 512], F32), SBring(st, 2, [128, 8], F32), SBring(st, 2, [128, 8], F32))
                    onr = SBring(st, 1, [128, 512], F32); sgr = SBring(st, 1, [128, 512], F32); oar = SBring(st, 1, [128, 512], BF16)
                    oaTr = SBring(st, 1, [128, 4, 128], BF16)
                    m1r = SBring(st, 1, [128, 512], F32); m2r = SBring(st, 1, [128, 512], F32); mgr = SBring(st, 1, [128, D], BF16)
                    mgTr = SBring(st, 1, [128, KC, 128], BF16); htr = SBring(st, 1, [128, D], F32)
                    nb2 = norm_bufs(st)
                    Wa = SB(st, [128, 4, D], BF16); ldw(Wa[:], w_a.rearrange("(c p) n -> p c n", p=128), [Wa])
                    Wb = SB(st, [128, 4, D], BF16)
                    wb64 = w_b.rearrange("(c p) n -> p c n", p=64)
                    ldw(Wb[0:64, :, :], wb64[:, 0:4, :], [Wb])
                    ldw(Wb[64:128, :, :], wb64[:, 4:8, :], [Wb])
                    Wo = SB(st, [128, KC, D], BF16); ldw(Wo[:], w_out.rearrange("(c p) n -> p c n", p=128), [Wo])
                S32 = S32r.next(); Sbf = Sbfr.next()
                cp("dve", S32[:], Sin[dr][:], [Sin[dr]], [S32])
                cp("act", Sbf[:], Sin[dr][:], [Sin[dr]], [Sbf])
                st_cur = [S32, Sbf]
                tiles = list(range(NE) if dr == "f" else range(NE - 1, -1, -1))

                def frontA(t):
                    rsl = slice(t * 128, (t + 1) * 128)
                    hq = PJ.next(); hi = PI.next(); hf = PFr.next()
                    ld(hq[:], PROJ[rsl, 0:512], [hq], [r_PROJ[t]], key="%shq%d" % (dr, t % 2))
                    ld(hi[:], PROJ[rsl, 512:1024], [hi], [r_PROJ[t]], key="%shi%d" % (dr, t % 2))
                    ld(hf[:], PROJ[rsl, c_hf:c_hf + 512], [hf], [r_PROJ[t]], key="%shf%d" % (dr, t % 2))
                    qs = qsr.next(); sn = snr.next(); k = kr_.next(); kb = kbr.next(); g = gr.next(); v = vr.next()
                    act_sigmoid(sn[:], hf[:], [hf], sn, True)
                    tt("dve", k[:], sn[:], oml[dr][:], ALU.mult, [sn, oml[dr]], [k])
                    tt("dve", kb[:], sn[:], oml[dr][:], ALU.mult, [sn, oml[dr]], [kb])
                    act(g[:], k[:], AF.Ln, [k], [g], scale=-1.0, bias=1.0)
                    qsg = qsgr.next()
                    act_sigmoid(qsg[:], hq[:], [hq], qsg, False)
                    tt("dve", qs[:], hq[:], qsg[:], ALU.mult, [hq, qsg], [qs])
                    cp("dve", v[:], hi[:], [hi], [v])
                    pq = psB.next()
                    for hh in range(4):
                        tr(pq[:, hh, :], qs[:, hh * 128:(hh + 1) * 128], identb[:], [qs, identb], [pq])
                        tr(pq[:, 4 + hh, :], kb[:, hh * 128:(hh + 1) * 128], identb[:], [kb, identb], [pq])
                    ps_bT = psF.next(); ps_sf = psF.next(); ps_dc = psF.next()
                    for hh in range(4):
                        mm(ps_bT[:, hh * 128:(hh + 1) * 128], g[:, hh * 128:(hh + 1) * 128], trif[:, ti_inc, :], [g, trif], [ps_bT])
                    mm(ps_sf[:], trif[:, ti_suf, :], g[:], [trif, g], [ps_sf])
                    for hh in range(4):
                        mm(ps_dc[:, hh * 2:hh * 2 + 2], g[:, hh * 128:(hh + 1) * 128], cindf[:], [g, cindf], [ps_dc])
                    return dict(t=t, rsl=rsl, k=k, v=v, pq=pq, ps_bT=ps_bT, ps_sf=ps_sf, ps_dc=ps_dc)

                def frontB(fa):
                    k, v, pq, ps_bT, ps_sf, ps_dc = fa["k"], fa["v"], fa["pq"], fa["ps_bT"], fa["ps_sf"], fa["ps_dc"]
                    E1 = E1r.next(); E2 = E2r.next(); E3 = E3r.next(); de = der.next()
                    act(E1[:].rearrange("p a b -> p (a b)"), ps_bT[:], AF.Exp, [ps_bT], [E1])
                    act(E2[:].rearrange("p a b -> p (a b)"), ps_bT[:], AF.Exp, [ps_bT], [E2], scale=-1.0)
                    act(E3[:], ps_sf[:], AF.Exp, [ps_sf], [E3])
                    act(de[:].rearrange("p a b -> p (a b)"), ps_dc[:, 0:8], AF.Exp, [ps_dc], [de])
                    qd = qdr.next(); kd = kdr.next(); kc = kcr.next()
                    tt("dve", qd[:], pq[:, 0:4, :], E1[:], ALU.mult, [pq, E1], [qd])
                    tt("dve", kd[:], pq[:, 4:8, :], E2[:], ALU.mult, [pq, E2], [kd])
                    tt("dve", kc[:], k[:], E3[:], ALU.mult, [k, E3], [kc])
                    ps_A = psF.next()
                    for hh in range(4):
                        mm(ps_A[:, hh * 128:(hh + 1) * 128], kd[:, hh, :], qd[:, hh, :], [kd, qd], [ps_A])
                    at = atr.next()
                    tt("dve", at[:].rearrange("p a b -> p (a b)"), ps_A[:], mask4[dr][:].rearrange("p a b -> p (a b)"),
                       ALU.mult, [ps_A, mask4[dr]], [at])
                    ps_o = psF.next()
                    for hh in range(4):
                        mm(ps_o[:, hh * 128:(hh + 1) * 128], at[:, hh, :], v[:, hh * 128:(hh + 1) * 128], [at, v], [ps_o])
                    oi = oir.next()
                    cp("act", oi[:], ps_o[:], [ps_o], [oi])
                    return dict(t=fa["t"], rsl=fa["rsl"], qd=qd, kc=kc, v=v, de=de, oi=oi)

                def back_chunk(cur, ps_i, cc):
                    qd, kc, v, de = cur["qd"], cur["kc"], cur["v"], cur["de"]
                    S32, Sbf = st_cur[0], st_cur[1]
                    rws = rows[cc]
                    for hh in range(4):
                        mm(ps_i[rws, hh * 128:(hh + 1) * 128], qd[:, hh, rws], Sbf[:, hh, :], [qd, Sbf], [ps_i])
                    ps_U = psF.next()
                    for hh in range(4):
                        mm(ps_U[:, hh * 128:(hh + 1) * 128], kc[rws, hh * 128:(hh + 1) * 128], v[rws, hh * 128:(hh + 1) * 128],
                           [kc, v], [ps_U])
                    S32n = S32r.next(); Sbfn = Sbfr.next()
                    for hh in range(4):
                        stt(S32n[:, hh, :], S32[:, hh, :], de[:, hh, cc:cc + 1], ps_U[:, hh * 128:(hh + 1) * 128],
                            ALU.mult, ALU.add, [S32, de, ps_U], [S32n])
                    cp("act", Sbfn[:], S32n[:], [S32n], [Sbfn])
                    st_cur[0], st_cur[1] = S32n, Sbfn

                fr_next = frontB(frontA(tiles[0]))
                for ti_, t in enumerate(tiles):
                    cur = fr_next
                    nxt = tiles[ti_ + 1] if ti_ + 1 < len(tiles) else None
                    rsl, oi = cur["rsl"], cur["oi"]
                    ps_i = psF.next(); oo = oor.next()
                    back_chunk(cur, ps_i, order[0])
                    fa = frontA(nxt) if nxt is not None else None
                    back_chunk(cur, ps_i, order[1])
                    tt("dve", oo[:], oi[:], ps_i[:], ALU.add, [oi, ps_i], [oo])
                    if fa is not None:
                        fr_next = frontB(fa)
                    if not merge:
                        ld(OBWD[rsl, :], oo[:], [r_OBWD[t]], [oo], key="obw%d" % (t % 2))
                        continue
                    ob = obr.next(); hg = PGr.next(); xe = xer.next()
                    ld(ob[:], OBWD[rsl, :], [ob], [r_OBWD[t]], key="lob%d" % (t % 2))
                    ld(hg[:], PROJ[rsl, 2048:2560], [hg], [r_PROJ[t]], key="lhg%d" % (t % 2))
                    ld(xe[:], xext[rsl, :], [xe], key="lxe")
                    tt("pool", oo[:], oo[:], ob[:], ALU.add, [oo, ob], [oo])
                    on = onr.next(); sg = sgr.next(); oa = oar.next()
                    head_rms(hr, oo, 4, 128, onb, on)
                    act_sigmoid(sg[:], hg[:], [hg], sg, False)
                    tt("dve", sg[:], sg[:], hg[:], ALU.mult, [sg, hg], [sg])
                    tt("dve", oa[:], on[:], sg[:], ALU.mult, [on, sg], [oa])
                    if debug:
                        out_deps.append(ld(DBG_OA[rsl, :], on[:], [r_dbg], [on], key="dboa"))
                    pT = psB.next(); oaT = oaTr.next()
                    for hh in range(4):
                        tr(pT[:, hh, :], oa[:, hh * 128:(hh + 1) * 128], identb[:], [oa, identb], [pT])
                    cp("act", oaT[:], pT[:, 0:4, :], [pT], [oaT])
                    mg = mgr.next()
                    for hf_ in range(2):
                        cs = slice(hf_ * 512, (hf_ + 1) * 512)
                        ga = GAr.next(); gb_ = GBr.next(); m1 = m1r.next(); m2 = m2r.next()
                        ld(ga[:], PROJ[rsl, 3072 + hf_ * 512:3072 + (hf_ + 1) * 512], [ga], [r_PROJ[t]], key="lga%d" % hf_)
                        ld(gb_[:], PROJ[rsl, 4096 + hf_ * 512:4096 + (hf_ + 1) * 512], [gb_], [r_PROJ[t]], key="lgb%d" % hf_)
                        act_sigmoid(ga[:], ga[:], [ga], ga, False)
                        act_sigmoid(gb_[:], gb_[:], [gb_], gb_, False)
                        ps_a = psF.next()
                        for c in range(4):
                            mm(ps_a[:], oaT[:, c, :], Wa[:, c, cs], [oaT, Wa], [ps_a], start=(c == 0), stop=(c == 3))
                        tt("dve", m1[:], ps_a[:], ga[:], ALU.mult, [ps_a, ga], [m1])
                        ps_b2 = psF.next()
                        for jj in range(4):
                            mm(ps_b2[:], obT[:, jj, rsl], Wb[:, jj, cs], [obT, Wb], [ps_b2], start=(jj == 0), stop=(jj == 3))
                        tt("dve", m2[:], ps_b2[:], gb_[:], ALU.mult, [ps_b2, gb_], [m2])
                        tt("pool", mg[:, cs], m1[:], m2[:], ALU.add, [m1, m2], [mg])
                    pT = psB.next(); mgT = mgTr.next()
                    for c in range(KC):
                        tr(pT[:, c, :], mg[:, c * 128:(c + 1) * 128], identb[:], [mg, identb], [pT])
                    cp("act", mgT[:], pT[:], [pT], [mgT])
                    ht = htr.next()
                    for hf_ in range(2):
                        cs = slice(hf_ * 512, (hf_ + 1) * 512)
                        ps_h = psF.next()
                        for c in range(KC):
                            mm(ps_h[:], mgT[:, c, :], Wo[:, c, cs], [mgT, Wo], [ps_h], start=(c == 0), stop=(c == KC - 1))
                        tt("dve", ht[:, cs], ps_h[:], xe[:, cs], ALU.add, [ps_h, xe], [ht])
                    lo = max(0, 64 - t * 128); hi_ = min(128, OWN + 64 - t * 128)
                    if hi_ > lo:
                        y0 = t * 128 - 64 + lo
                        yres = [r_y[i] for i in sorted(set([y0 // 128, (y0 + hi_ - lo - 1) // 128]))]
                        ld(y[y0:y0 + (hi_ - lo), :], ht[lo:hi_, :], yres, [ht], key="yh")
                    if debug:
                        out_deps.append(ld(DBG_H[rsl, :], ht[:], [r_dbg], [ht], key="dbh"))
                    hnT = rmsnorm_T(nb2, ht, g2b)
                    ld(HNT.rearrange("(c p) n -> p c n", p=128)[:, :, rsl], hnT[:], [r_HNT[t]], [hnT], key="hnt%d" % (t % 2))

            if stop == "P3":
                raise _Stop()
            with contextlib.ExitStack() as S4:
                hgrn_sweep("b", S4, False)
                P.barrier()
            if stop == "P4":
                raise _Stop()
            with contextlib.ExitStack() as S5:
                hgrn_sweep("f", S5, True)
                P.barrier()
            if stop == "P5":
                raise _Stop()

            with contextlib.ExitStack() as S6:
                Wd = SB(S6, [128, 22, D], BF16)
                ldw(Wd[:, 0:11, :], w_down.rearrange("(c p) n -> p c n", p=128)[:, 0:11, :], [Wd])
                ldw(Wd[:, 11:22, :], w_down.rearrange("(c p) n -> p c n", p=128)[:, 11:22, :], [Wd])
                hbr = SBring(S6, 2, [128, KC, 514], BF16, "hb")
                wur = SBring(S6, 4, [128, KC, 2, 128], BF16, "wu")
                upr = SBring(S6, 4, [128, 514], F32, "up")
                c1r = SBring(S6, 1, [128, 512], F32); c2r = SBring(S6, 1, [128, 512], F32)
                g1r = SBring(S6, 1, [128, 512], F32); g2r = SBring(S6, 1, [128, 512], F32); sgr = SBring(S6, 1, [128, 512], F32)
                aTr = SBring(S6, 1, [128, 22, 512], BF16, "aT")
                hrr = SBring(S6, 2, [128, D], F32, "hres"); outr = SBring(S6, 2, [128, D], F32, "out")
                nblk = OWN // 512
                for b in range(nblk):
                    c0 = 64 + 512 * b
                    hb = hbr.next()
                    hres_list = [r_HNT[i] for i in range((c0 - 1) // 128, (c0 + 512) // 128 + 1)]
                    ld(hb[:], HNT.rearrange("(c p) n -> p c n", p=128)[:, :, c0 - 1:c0 + 513], [hb], hres_list, key="hb%d" % (b % 2))
                    if b == 0:
                        ts("dve", hb[:, :, 0:1], hb[:, :, 0:1], cvl[:, 0:1], None, ALU.mult, None, [hb, cvl], [hb])
                    if b == nblk - 1:
                        ts("dve", hb[:, :, 513:514], hb[:, :, 513:514], cvl[:, 1:2], None, ALU.mult, None, [hb, cvl], [hb])
                    aT = aTr.next()
                    for c in range(22):
                        wuk = wur.i % 4
                        wu = wur.next()
                        ldw(wu[:, :, 0, :], w_up[c], [wu], key="wu%da" % wuk)
                        ldw(wu[:, :, 1, :], w_up[22 + c], [wu], key="wu%db" % wuk)
                        res = []
                        for vg in range(2):
                            ps_m = psF.next(); ps_e = psF.next()
                            for kc_ in range(KC):
                                mm(ps_m[:], wu[:, kc_, vg, :], hb[:, kc_, 1:513], [wu, hb], [ps_m], start=(kc_ == 0), stop=(kc_ == KC - 1))
                            for kc_ in range(KC):
                                mm(ps_e[:, 0:2], wu[:, kc_, vg, :], hb[:, kc_, 0:514:513], [wu, hb], [ps_e], start=(kc_ == 0), stop=(kc_ == KC - 1))
                            up = upr.next()
                            cp("act", up[:, 1:513], ps_m[:], [ps_m], [up])
                            cp("dve", up[:, 0:514:513], ps_e[:, 0:2], [ps_e], [up])
                            ch = c + 22 * vg
                            a1 = (c1r if vg == 0 else g1r).next(); a2 = (c2r if vg == 0 else g2r).next()
                            act(a1[:], ps_m[:], AF.Identity, [ps_m, cwt, cbt], [a1], scale=cwt[:, ch, 1:2], bias=cbt[:, ch:ch + 1])
                            stt(a2[:], up[:, 0:512], cwt[:, ch, 0:1], a1[:], ALU.mult, ALU.add, [up, cwt, a1], [a2])
                            stt(a1[:], up[:, 2:514], cwt[:, ch, 2:3], a2[:], ALU.mult, ALU.add, [up, cwt, a2], [a1])
                            res.append(a1)
                        sg = sgr.next()
                        act(sg[:], res[1][:], AF.Silu, [res[1]], [sg])
                        tt("dve", aT[:, c, :], sg[:], res[0][:], ALU.mult, [sg, res[0]], [aT])
                    for tt_ in range(4):
                        ot_i = b * 4 + tt_
                        row0 = ot_i * 128
                        hres = hrr.next(); ot = outr.next()
                        ld(hres[:], y[row0:row0 + 128, :], [hres], [r_y[ot_i]], key="hres%d" % (tt_ % 2))
                        for hf_ in range(2):
                            cs = slice(hf_ * 512, (hf_ + 1) * 512)
                            ps_f2 = psF.next()
                            for c in range(22):
                                mm(ps_f2[:], aT[:, c, tt_ * 128:(tt_ + 1) * 128], Wd[:, c, cs], [aT, Wd], [ps_f2], start=(c == 0), stop=(c == 21))
                            tt("dve", ot[:, cs], ps_f2[:], hres[:, cs], ALU.add, [ps_f2, hres], [ot])
                        out_deps.append(ld(y[row0:row0 + 128, :], ot[:], [r_y[ot_i]], [ot], key="yo%d" % (tt_ % 2)))
        except _Stop:
            P.barrier()
            for st_ in reversed(open_stacks):
                st_.close()
        P.build(final_deps=out_deps)
    return nc


def _rope_tables(pos_ids):
    pos = np.asarray(pos_ids)
    pr = (pos // 64).astype(np.float32)
    pc = (pos % 64).astype(np.float32)
    inv = (np.float32(10000.0) ** (-np.arange(0, 32, 2, dtype=np.float32) / np.float32(32))).astype(np.float32)

    def tab(p):
        ang = (p[:, None] * inv[None, :]).astype(np.float32)
        ang = np.concatenate([ang, ang], axis=-1)
        return np.cos(ang).astype(np.float32), np.sin(ang).astype(np.float32)

    cr, sr = tab(pr)
    cc, sc = tab(pc)
    C = np.concatenate([cr, cc], axis=-1)
    S = np.concatenate([sr, sc], axis=-1)
    sign = np.tile(np.concatenate([-np.ones(16, np.float32), np.ones(16, np.float32)]), 2)
    return C, (S * sign[None, :]).astype(np.float32)


def _consts():
    s = np.arange(128)[:, None]
    c = np.arange(128)[None, :]
    same = (s // 64) == (c // 64)
    mats = [same & (s <= c), same & (s >= c), same & (s > c), same & (s < c), s > c, s < c]
    tri = np.stack([m.astype(np.float32) for m in mats], axis=1)
    cind = np.stack([(np.arange(128) < 64), (np.arange(128) >= 64)], axis=1).astype(np.float32)
    return np.ascontiguousarray(tri), cind, np.eye(128, dtype=np.float32)


_CACHE = {}


def run(inputs, SEQ, debug=False, stop=None):
    f = lambda a: np.ascontiguousarray(np.asarray(a, dtype=np.float32))
    x = f(inputs["x"])
    B = x.shape[0]
    NA = SEQ // 128
    OWN = SEQ // 4
    NE = OWN // 128 + 1
    NEXT = NE * 128
    tri, cind, ident = _consts()
    Ck, Sk = _rope_tables(np.arange(SEQ))
    shared = {
        "cosk": np.ascontiguousarray(np.tile(Ck, (1, 2))), "sink": np.ascontiguousarray(np.tile(Sk, (1, 2))),
        "tri": tri, "cind": cind, "ident": ident,
        "w_in": f(inputs["w_in"][0]), "w_a": f(inputs["w_branch_a"][0]), "w_b": f(inputs["w_branch_b"][0]),
        "w_out": f(inputs["w_out"][0]), "w_down": f(inputs["w_down"][0]),
        "w_up": np.ascontiguousarray(f(inputs["w_up"][0]).reshape(KC, 128, 44, 128).transpose(2, 1, 0, 3)),
        "g1": f(inputs["norm1_g"][0:1]), "g2": f(inputs["norm2_g"][0:1]),
        "lbf": f(inputs["hg_lb_fwd"]), "lbb": f(inputs["hg_lb_bwd"]),
        "onorm": np.ascontiguousarray(np.tile(f(inputs["hg_onorm_g"][0:1]), (1, 4))),
        "qg": np.ascontiguousarray(np.tile(f(inputs["q_norm_g"][0:1]), (1, 8))),
        "kg": np.ascontiguousarray(np.tile(f(inputs["k_norm_g"][0:1]), (1, 2))),
        "cw": np.ascontiguousarray(f(inputs["conv_w"][0]).reshape(3, 44, 128).transpose(2, 1, 0)),
        "cb": np.ascontiguousarray(f(inputs["conv_b"][0]).reshape(44, 128).T),
    }
    in_maps = []
    for c in range(8):
        b, r = c // 4, c % 4
        s0 = r * OWN
        e0 = s0 - 64
        pos = np.arange(e0, e0 + NEXT)
        valid = (pos >= 0) & (pos < SEQ)
        xe = np.zeros((NEXT, D), np.float32)
        xe[valid] = x[b, pos[valid]]
        Cq, Sq = _rope_tables(np.clip(pos, 0, SEQ - 1))
        tok = np.arange(SEQ).reshape(NA, 128).T
        m = dict(shared)
        m.update({
            "xall": x[b], "xext": xe,
            "cosq": np.ascontiguousarray(np.tile(Cq, (1, 8))), "sinq": np.ascontiguousarray(np.tile(Sq, (1, 8))),
            "mpre": (tok < e0).astype(np.float32), "mpost": (tok >= e0 + NEXT).astype(np.float32),
            "cval": np.ascontiguousarray(np.tile(np.array([[float(s0 > 0), float(s0 + OWN < SEQ)]], np.float32), (128, 1))),
        })
        in_maps.append(m)
    key = (SEQ, debug, stop)
    if key not in _CACHE:
        _CACHE[key] = build_program(SEQ, debug, stop)
    nc = _CACHE[key]
    res = run_bass_kernel_spmd(nc, in_maps, core_ids=list(range(8)))
    out = np.zeros((B, SEQ, D), np.float32)
    for c in range(8):
        b, r = c // 4, c % 4
        out[b, r * OWN:(r + 1) * OWN] = res.results[c]["y"]
    return out, res


def kernel(**inputs):
    out, _ = run(inputs, 8192)
    return out
```

```python
import contextlib
import numpy as np
import concourse.bass as bass
import concourse.mybir as mybir
from concourse.bass_utils import run_bass_kernel_spmd

F32 = mybir.dt.float32
BF16 = mybir.dt.bfloat16
ALU = mybir.AluOpType
AF = mybir.ActivationFunctionType
AX = mybir.AxisListType

import os
SKIP = os.environ.get("SKIP", "")
D = 1024
KC = 8
DFF = 2816
EPS = 1e-6
C_HQ, C_HI, C_FF, C_FB, C_HG, C_AQ, C_AK, C_AV, C_GA, C_GB = 0, 512, 1024, 1536, 2048, 2560, 3072, 3200, 3328, 4352


class Res:
    __slots__ = ("w", "r")

    def __init__(self):
        self.w = {}
        self.r = {}


class Buf:
    def __init__(self, t):
        self.t = t
        self.res = Res()

    def __getitem__(self, idx):
        return self.t[idx]


def _res(x):
    return x.res if isinstance(x, Buf) else x


class Prog:
    ENGS = ("pe", "act", "dve", "pool", "sp")

    def __init__(self, nc):
        self.nc = nc
        self.ops = {e: [] for e in self.ENGS}
        self.dma_sems = {}
        self.pending = {}

    def _collect(self, eng, reads, writes):
        deps = list(self.pending.pop(eng, []))
        for r in reads:
            deps.extend(_res(r).w.values())
        for w in writes:
            w = _res(w)
            for d in list(w.w.values()) + list(w.r.values()):
                if d[0] == "e" and d[1] == eng and (eng == "pe" or os.environ.get("NOSELF")):
                    continue
                deps.append(d)
        return deps

    def op(self, eng, thunk, reads=(), writes=()):
        deps = self._collect(eng, reads, writes)
        idx = len(self.ops[eng])
        self.ops[eng].append([deps, thunk, None])
        if os.environ.get("DUMP"):
            import sys as _s
            f = _s._getframe(1); ln = []
            while f is not None and len(ln) < 3:
                ln.append(f.f_lineno); f = f.f_back
            self.ops[eng][-1].append(ln)
        me = ("e", eng, idx)
        for r in reads:
            _res(r).r[("e", eng)] = me
        for w in writes:
            w = _res(w)
            w.w = {("e", eng): me}
            w.r = {}
        return me

    def dma(self, queue, thunk, semkey, reads=(), writes=()):
        deps = self._collect(queue, reads, writes)
        c = self.dma_sems.setdefault(semkey, [0])
        c[0] += 16
        self.ops[queue].append([deps, thunk, semkey])
        if os.environ.get("DUMP"):
            import sys as _s
            f = _s._getframe(1); ln = []
            while f is not None and len(ln) < 3:
                ln.append(f.f_lineno); f = f.f_back
            self.ops[queue][-1].append(ln)
        me = ("d", semkey, c[0])
        for r in reads:
            _res(r).r[("d", semkey)] = me
        for w in writes:
            w = _res(w)
            w.w = {("d", semkey): me}
            w.r = {}
        return me

    def barrier(self):
        deps = []
        for e in self.ENGS:
            for i in range(len(self.ops[e]) - 1, -1, -1):
                if self.ops[e][i][2] is None:
                    deps.append(("e", e, i))
                    break
        for k, c in self.dma_sems.items():
            deps.append(("d", k, c[0]))
        for e in self.ENGS:
            self.pending[e] = list(self.pending.get(e, [])) + deps

    def build(self, final_deps=()):
        nc = self.nc
        final_deps = list(final_deps) + [("d", k, c[0]) for k, c in self.dma_sems.items()]
        needed = {e: set() for e in self.ENGS}
        for e in self.ENGS:
            for idx, rec in enumerate(self.ops[e]):
                for d in rec[0]:
                    if d[0] == "e" and not (d[1] == e and d[2] >= idx):
                        needed[d[1]].add(d[2])
        for d in final_deps:
            if d[0] == "e":
                needed[d[1]].add(d[2])
        semval = {}
        for e in self.ENGS:
            for rank, idx in enumerate(sorted(needed[e])):
                semval[(e, idx)] = rank + 1
        with contextlib.ExitStack() as st:
            esem = {e: st.enter_context(nc.semaphore("s_" + e)) for e in self.ENGS}
            print("n dma sems", len(self.dma_sems), {e: len(self.ops[e]) for e in self.ENGS})
            dsem = {k: st.enter_context(nc.semaphore("d_%d" % i)) for i, k in enumerate(self.dma_sems)}
            block = st.enter_context(nc.Block())

            def replay(e, h):
                waited = {}
                for idx, rec in enumerate(self.ops[e]):
                    deps, thunk, dk = rec[0], rec[1], rec[2]
                    dump = []
                    for d in deps:
                        if d[0] == "e":
                            if d[1] == e and d[2] >= idx:
                                continue
                            key = ("e", d[1]); sem = esem[d[1]]; val = semval[(d[1], d[2])]
                        else:
                            key = ("d", d[1]); sem = dsem[d[1]]; val = d[2]
                        if waited.get(key, 0) >= val:
                            continue
                        waited[key] = val
                        h.wait_ge(sem, val)
                        dump.append((key[1], val))
                    if os.environ.get("DUMP"):
                        print("OP", e, idx, "lines", rec[3], "waits", dump, "dma" if dk else "", "inc", semval.get((e, idx)))
                    ins = thunk(h)
                    if dk is not None:
                        ins.then_inc(dsem[dk], 16)
                    elif (e, idx) in semval:
                        ins.then_inc(esem[e], 1)
                if e == "sp":
                    for d in final_deps:
                        if d[0] == "e":
                            h.wait_ge(esem[d[1]], semval[(d[1], d[2])])
                        else:
                            h.wait_ge(dsem[d[1]], d[2])

            @block.tensor
            def _(h):
                replay("pe", h)

            @block.scalar
            def _(h):
                replay("act", h)

            @block.vector
            def _(h):
                replay("dve", h)

            @block.gpsimd
            def _(h):
                replay("pool", h)

            @block.sync
            def _(h):
                replay("sp", h)


class Ring:
    def __init__(self, bufs):
        self.bufs = bufs
        self.i = 0

    def next(self):
        b = self.bufs[self.i % len(self.bufs)]
        self.i += 1
        return b


class _Stop(Exception):
    pass


def build_program(SEQ, debug=False, stop=None):
    NA = SEQ // 128
    OWN = SEQ // 4
    NE = OWN // 128 + 1
    NEXT = NE * 128
    nc = bass.Bass("TRN2", target_bir_lowering=False)

    def din(name, shape, dt=F32):
        return nc.dram_tensor(name, list(shape), dt, kind="ExternalInput").ap()

    xall = din("xall", [SEQ, D]); xext = din("xext", [NEXT, D])
    cosk = din("cosk", [SEQ, 128]); sink = din("sink", [SEQ, 128])
    cosq = din("cosq", [NEXT, 512]); sinq = din("sinq", [NEXT, 512])
    mpre = din("mpre", [128, NA]); mpost = din("mpost", [128, NA]); cval = din("cval", [128, 2])
    tri = din("tri", [128, 6, 128]); cind = din("cind", [128, 2]); ident = din("ident", [128, 128])
    w_in = din("w_in", [D, 5376]); w_a = din("w_a", [512, D]); w_b = din("w_b", [512, D])
    w_out = din("w_out", [D, D]); w_up = din("w_up", [44, 128, KC, 128]); w_down = din("w_down", [DFF, D])
    g1 = din("g1", [1, D]); g2 = din("g2", [1, D]); lbf = din("lbf", [2, 512]); lbb = din("lbb", [2, 512])
    onorm = din("onorm", [1, 512]); qg = din("qg", [1, 512]); kg = din("kg", [1, 128])
    cw = din("cw", [128, 44, 3]); cb = din("cb", [128, 44])
    y = nc.dram_tensor("y", [OWN, D], F32, kind="ExternalOutput").ap()
    skind = "ExternalOutput" if debug else "Internal"
    PROJ = nc.dram_tensor("PROJ", [NEXT, 5120], F32, kind=skind).ap()
    OBWD = nc.dram_tensor("OBWD", [NEXT, 512], F32, kind=skind).ap()
    HNT = nc.dram_tensor("HNT", [D, NEXT], BF16, kind=skind).ap()
    if debug:
        DBG_OB = nc.dram_tensor("DBG_OB", [128, 4, NEXT], BF16, kind="ExternalOutput").ap()
        DBG_OA = nc.dram_tensor("DBG_OA", [NEXT, 512], F32, kind="ExternalOutput").ap()
        DBG_S = nc.dram_tensor("DBG_S", [128, 2, 512], F32, kind="ExternalOutput").ap()
        DBG_H = nc.dram_tensor("DBG_H", [NEXT, D], F32, kind="ExternalOutput").ap()
    r_PROJ = [Res() for _ in range(NE)]; r_OBWD = [Res() for _ in range(NE)]; r_HNT = [Res() for _ in range(NE)]
    r_y = [Res() for _ in range(OWN // 128)]; r_dbg = Res()

    P = Prog(nc)
    out_deps = []
    uid = [0]

    def nm(p):
        uid[0] += 1
        return "%s%d" % (p, uid[0])

    with contextlib.ExitStack() as S0:
        sb_live = [0, 0]

        def SB(st, shape, dt, name="b"):
            nbytes = int(np.prod(shape[1:])) * (2 if dt == BF16 else 4)
            nbytes = (nbytes + 31) // 32 * 32

            def _free(nbytes=nbytes):
                sb_live[0] -= nbytes
            st.callback(_free)
            sb_live[0] += nbytes
            if sb_live[0] > sb_live[1]:
                sb_live[1] = sb_live[0]
                if os.environ.get("SBDBG"):
                    print("SBUF high-water", sb_live[1], "at", name)
            return Buf(st.enter_context(nc.sbuf_tensor(nm(name), list(shape), dt)))

        def SBring(st, n, shape, dt, name="r"):
            return Ring([SB(st, shape, dt, name) for _ in range(n)])

        psF = Ring([Buf(S0.enter_context(nc.psum_tensor(nm("pf"), [128, 512], F32))) for _ in range(6)])
        psB = Ring([Buf(S0.enter_context(nc.psum_tensor(nm("pb"), [128, 8, 128], BF16))) for _ in range(2)])

        def act(out, in_, func, rd, wr, **kw):
            P.op("act", lambda h: h.activation(out=out, in_=in_, func=func, **kw), rd, wr)

        def tt(eng, out, in0, in1, op, rd, wr):
            P.op(eng, lambda h: h.tensor_tensor(out=out, in0=in0, in1=in1, op=op), rd, wr)

        def ts(eng, out, in0, s1, s2, op0, op1, rd, wr):
            if op1 is None:
                P.op(eng, lambda h: h.tensor_scalar(out=out, in0=in0, scalar1=s1, scalar2=None, op0=op0), rd, wr)
            else:
                P.op(eng, lambda h: h.tensor_scalar(out=out, in0=in0, scalar1=s1, scalar2=s2, op0=op0, op1=op1), rd, wr)

        def stt(out, in0, scalar, in1, op0, op1, rd, wr):
            P.op("dve", lambda h: h.scalar_tensor_tensor(out=out, in0=in0, scalar=scalar, in1=in1, op0=op0, op1=op1), rd, wr)

        def cp(eng, out, in_, rd, wr):
            if eng == "act":
                P.op("act", lambda h: h.copy(out=out, in_=in_), rd, wr)
            else:
                P.op(eng, lambda h: h.tensor_copy(out=out, in_=in_), rd, wr)

        def act_sigmoid(out_ap, in_ap, rd_in, buf, neg):
            act(out_ap, in_ap, AF.Exp, rd_in, [buf], scale=(1.0 if neg else -1.0))
            act(out_ap, out_ap, AF.Ln, [buf], [buf], bias=1.0)
            act(out_ap, out_ap, AF.Exp, [buf], [buf], scale=-1.0)

        def mm(out, lhsT, rhs, rd, wr, start=True, stop=True):
            P.op("pe", lambda h: h.matmul(out, lhsT=lhsT, rhs=rhs, start=start, stop=stop), rd, wr)

        def tr(out, in_, idn, rd, wr):
            P.op("pe", lambda h: h.transpose(out=out, in_=in_, identity=idn), rd, wr)

        dq = [0]

        setup = [True]
        r_wl = Res()

        def ld(out, in_, wr, rd=(), key=None, q="sp"):
            return P.dma(q, lambda h: h.dma_start(out=out, in_=in_), key or ("setupL" if setup[0] else nm("L")), rd, wr)

        def ldw(out, in_, wr, key=None):
            if key is None:
                return P.dma("pool", lambda h: h.dma_start(out=out, in_=in_), "wload", (), list(wr) + [r_wl])
            return P.dma("pool", lambda h: h.dma_start(out=out, in_=in_), key, (), wr)

        identb = SB(S0, [128, 128], BF16); ldw(identb[:], ident, [identb])
        trif = SB(S0, [128, 6, 128], F32); ld(trif[:], tri, [trif])
        T_INCF, T_INCB, T_SUFF64, T_SUFB64, T_SUFF128, T_SUFB128 = range(6)
        cindf = SB(S0, [128, 2], F32); ld(cindf[:], cind, [cindf])
        onesf = SB(S0, [128, 128], F32)
        P.op("dve", lambda h: h.memset(onesf[:], 1.0), (), [onesf])
        mask4 = {}
        for nmk, ti in (("f", T_INCF), ("b", T_INCB)):
            m4 = SB(S0, [128, 4, 128], F32)
            for hh in range(4):
                ld(m4[:, hh, :], tri[:, ti, :], [m4])
            mask4[nmk] = m4
        g1b = SB(S0, [128, D], F32); ld(g1b[:], g1.broadcast_to([128, D]), [g1b])
        g2b = SB(S0, [128, D], F32); ld(g2b[:], g2.broadcast_to([128, D]), [g2b])
        onb = SB(S0, [128, 512], F32); ld(onb[:], onorm.broadcast_to([128, 512]), [onb])
        qgb = SB(S0, [128, 512], F32); ld(qgb[:], qg.broadcast_to([128, 512]), [qgb])
        kgb = SB(S0, [128, 128], F32); ld(kgb[:], kg.broadcast_to([128, 128]), [kgb])
        mpr = SB(S0, [128, NA], F32); ld(mpr[:], mpre, [mpr])
        mpo = SB(S0, [128, NA], F32); ld(mpo[:], mpost, [mpo])
        cvl = SB(S0, [128, 2], F32); ld(cvl[:], cval, [cvl])
        cwt = SB(S0, [128, 44, 3], F32); ld(cwt[:], cw, [cwt])
        cbt = SB(S0, [128, 44], F32); ld(cbt[:], cb, [cbt])
        oml = {"f": SB(S0, [128, 512], F32), "b": SB(S0, [128, 512], F32)}
        with contextlib.ExitStack() as St:
            for nmk, lb in (("f", lbf), ("b", lbb)):
                a0 = SB(St, [128, 512], F32); a1 = SB(St, [128, 512], F32); om = oml[nmk]
                ld(a0[:], lb[0:1, :].broadcast_to([128, 512]), [a0], key="a0" + nmk)
                ld(a1[:], lb[1:2, :].broadcast_to([128, 512]), [a1], key="a1" + nmk)
                tt("dve", a1[:], a1[:], a0[:], ALU.subtract, [a0, a1], [a1])
                act(om[:], a1[:], AF.Sigmoid, [a1], [om])
            P.barrier()
        setup[0] = False
        obT = SB(S0, [128, 4, NEXT], BF16)
        Sin = {"f": SB(S0, [128, 4, 128], F32), "b": SB(S0, [128, 4, 128], F32)}
        Pd = SB(S0, [128, 4], F32)
        for k_ in ("f", "b"):
            P.op("dve", lambda h, k_=k_: h.memset(Sin[k_][:], 0.0), (), [Sin[k_]])
        P.op("dve", lambda h: h.memset(Pd[:], 1.0), (), [Pd])

        def rms_part1(st_bufs, xt, gb):
            ssr, rsr, xnr, uTr = st_bufs
            ss = ssr.next(); rs = rsr.next(); xn = xnr.next()
            P.op("act", lambda h: h.activation(out=xn[:], in_=xt[:], func=AF.Square, accum_out=ss[:]), [xt], [xn, ss])
            act(rs[:], ss[:], AF.Ln, [ss], [rs], scale=1.0 / D, bias=EPS)
            act(rs[:], rs[:], AF.Exp, [rs], [rs], scale=-0.5)
            stt(xn[:], xt[:], rs[:, 0:1], gb[:], ALU.mult, ALU.mult, [xt, rs, gb], [xn])
            return xn

        def rms_part2(st_bufs, xn):
            uT = st_bufs[3].next()
            pT = psB.next()
            for c in range(KC):
                tr(pT[:, c, :], xn[:, c * 128:(c + 1) * 128], identb[:], [xn, identb], [pT])
            cp("act", uT[:], pT[:], [pT], [uT])
            return uT

        def rmsnorm_T(st_bufs, xt, gb):
            return rms_part2(st_bufs, rms_part1(st_bufs, xt, gb))

        def norm_bufs(st):
            return (SBring(st, 2, [128, 1], F32), SBring(st, 2, [128, 1], F32),
                    SBring(st, 2, [128, D], BF16), SBring(st, 2, [128, KC, 128], BF16))

        def proj_tok(ps, uT, w, c0, ncols):
            for c in range(KC):
                mm(ps[:, 0:ncols], uT[:, c, :], w[:, c, c0:c0 + ncols], [uT, w], [ps], start=(c == 0), stop=(c == KC - 1))

        def head_rms(st_r, src, nh, hd, gbt, out):
            HRM = os.environ.get("HRM", "")
            sq, ssr, rsr = st_r
            q2 = sq.next(); s2 = ssr.next(); r2 = rsr.next()
            tt("dve" if HRM == "nopool" else "pool", q2[:, 0:nh * hd], src[:, 0:nh * hd], src[:, 0:nh * hd], ALU.mult, [src], [q2])
            if HRM == "onlysq":
                return
            P.op("dve", lambda h: h.tensor_reduce(out=s2[:, 0:nh], in_=q2[:, 0:nh * hd].rearrange("p (a b) -> p a b", b=hd),
                                                  axis=AX.X, op=ALU.add), [q2], [s2])
            if HRM == "onlyred":
                return
            act(r2[:, 0:nh], s2[:, 0:nh], AF.Ln, [s2], [r2], scale=1.0 / hd, bias=EPS)
            act(r2[:, 0:nh], r2[:, 0:nh], AF.Exp, [r2], [r2], scale=-0.5)
            if HRM == "nostt":
                return
            for hh in range(nh):
                stt(out[:, hh * hd:(hh + 1) * hd], src[:, hh * hd:(hh + 1) * hd], r2[:, hh:hh + 1],
                    gbt[:, hh * hd:(hh + 1) * hd], ALU.mult, ALU.mult, [src, r2, gbt], [out])

        def rope_parts(st_r, xin, W, ct, sn_):
            t1 = st_r[0].next(); t2 = st_r[1].next()
            tt("dve", t1[:, 0:W], xin[:, 0:W], ct[:, 0:W], ALU.mult, [xin, ct], [t1])
            x4 = xin[:, 0:W].rearrange("p (a b c) -> p a b c", b=2, c=16)
            s4 = sn_[:, 0:W].rearrange("p (a b c) -> p a b c", b=2, c=16)
            o4 = t2[:, 0:W].rearrange("p (a b c) -> p a b c", b=2, c=16)
            tt("pool", o4[:, :, 0, :], x4[:, :, 1, :], s4[:, :, 0, :], ALU.mult, [xin, sn_], [t2])
            tt("pool", o4[:, :, 1, :], x4[:, :, 0, :], s4[:, :, 1, :], ALU.mult, [xin, sn_], [t2])
            return t1, t2

        open_stacks = []
        try:
            wv_ = w_in.rearrange("(c p) n -> p c n", p=128)
            SQ = contextlib.ExitStack(); open_stacks.append(SQ)
            QT = SB(SQ, [128, 8, NEXT], BF16)
            P.op("pool", lambda h: h.memset(QT[:].rearrange("p a b -> p (a b)"), 0.0), (), [QT])
            if os.environ.get("SBDBG"):
                print("live after QT", sb_live[0])
            for pas in range(2):
              with contextlib.ExitStack() as S2:
                ng = 6 if pas == 0 else 4
                wP = SB(S2, [128, KC, ng * 512], BF16)
                for gi in range(ng):
                    src0 = gi * 512 if pas == 0 else C_GA + gi * 512
                    ldw(wP[:, :, gi * 512:(gi + 1) * 512], wv_[:, :, src0:src0 + 512], [wP])
                nb = norm_bufs(S2)
                xr = SBring(S2, 2, [128, D], F32, "xe")
                cqr = SBring(S2, 2, [128, 512], F32); sqr = SBring(S2, 2, [128, 512], F32)
                evr = SBring(S2, 4, [128, 512], F32)
                qn_r = SBring(S2, 1, [128, 512], F32); qrb_r = SBring(S2, 2, [128, 512], BF16)
                hr = (SBring(S2, 1, [128, 512], F32), SBring(S2, 2, [128, 8], F32), SBring(S2, 2, [128, 8], F32))
                rr = (SBring(S2, 1, [128, 512], F32), SBring(S2, 1, [128, 512], F32))
                def p2_front(t):
                    xe = xr.next()
                    ld(xe[:], xext[t * 128:(t + 1) * 128, :], [xe], key="xe%d" % (t % 2))
                    return rms_part1(nb, xe, g1b)

                pend_q = []

                def flush_q():
                    while pend_q:
                        qrb_, t_ = pend_q.pop(0)
                        pT = psB.next()
                        for jj in range(4):
                            tr(pT[:, jj, :], qrb_[:, jj * 128:(jj + 1) * 128], identb[:], [qrb_, identb], [pT])
                        cp("act", QT[0:64, 0:4, t_ * 128:(t_ + 1) * 128], pT[0:64, 0:4, :], [pT], [QT])
                        cp("act", QT[64:128, 4:8, t_ * 128:(t_ + 1) * 128], pT[64:128, 0:4, :], [pT], [QT])

                xn_next = p2_front(0)
                for t in range(NE):
                    if pas == 0:
                        cq = cqr.next(); sq_ = sqr.next()
                        ld(cq[:], cosq[t * 128:(t + 1) * 128, :], [cq], key="cq%d" % (t % 2))
                        ld(sq_[:], sinq[t * 128:(t + 1) * 128, :], [sq_], key="sq%d" % (t % 2))
                    uT = rms_part2(nb, xn_next)
                    if t + 1 < NE:
                        xn_next = p2_front(t + 1)
                    for gi in range(ng):
                        if gi == 3:
                            flush_q()
                        ps = psF.next(); proj_tok(ps, uT, wP, gi * 512, 512)
                        evk = evr.i % 4
                        ev = evr.next()
                        cp("act" if gi % 2 == 0 else "dve", ev[:], ps[:], [ps], [ev])
                        gcol = gi * 512 if pas == 0 else 3072 + gi * 512
                        ld(PROJ[t * 128:(t + 1) * 128, gcol:gcol + 512], ev[:], [r_PROJ[t]], [ev], key="pj%d" % evk)
                        if pas == 0 and gi == 5:
                            qn = qn_r.next(); qrb = qrb_r.next()
                            head_rms(hr, ev, 8, 64, qgb, qn)
                            t1, t2 = rope_parts(rr, qn, 512, cq, sq_)
                            tt("dve", qrb[:].rearrange("p (j k d) -> p k j d", k=2, d=64),
                               t1[:].rearrange("p (k j d) -> p k j d", k=2, d=64),
                               t2[:].rearrange("p (k j d) -> p k j d", k=2, d=64), ALU.add, [t1, t2], [qrb])
                            pend_q.append((qrb, t))
                flush_q()
                P.barrier()
            if stop == "P2":
                raise _Stop()

            SKV = contextlib.ExitStack(); open_stacks.append(SKV)
            KT = SB(SKV, [128, SEQ], BF16)
            V1 = SB(SKV, [128, NA, 2, 128], BF16)
            P.op("pool", lambda h: h.memset(V1[:].rearrange("p a b c -> p (a b c)"), 0.0), (), [V1])
            P.op("pool", lambda h: h.memset(V1[:, :, 0, 64:65], 1.0), (), [V1])
            P.op("pool", lambda h: h.memset(V1[:, :, 1, 0:1], 1.0), (), [V1])
            with contextlib.ExitStack() as S1:
                wA = SB(S1, [128, KC, 1792], BF16)
                ldw(wA[:, :, 0:512], wv_[:, :, C_HI:C_HI + 512], [wA])
                ldw(wA[:, :, 512:1024], wv_[:, :, C_FF:C_FF + 512], [wA])
                ldw(wA[:, :, 1024:1536], wv_[:, :, C_FB:C_FB + 512], [wA])
                ldw(wA[:, :, 1536:1792], wv_[:, :, C_AK:C_AK + 256], [wA])
                nb = norm_bufs(S1)
                if os.environ.get("SBDBG"):
                    print("live in P1", sb_live[0])
                xr = SBring(S1, 2, [128, D], F32, "xa")
                ckr = SBring(S1, 2, [128, 128], F32); skr = SBring(S1, 2, [128, 128], F32)
                vr = SBring(S1, 2, [128, 512], BF16)
                kr_ = SBring(S1, 2, [128, 512], F32); gr = SBring(S1, 2, [128, 512], F32)
                Er = SBring(S1, 2, [128, 512], F32); kcr = SBring(S1, 2, [128, 512], BF16); decr = SBring(S1, 2, [128, 8], F32)
                kraw_r = SBring(S1, 2, [128, 128], F32); kn_r = SBring(S1, 2, [128, 128], F32); krb_r = SBring(S1, 2, [128, 128], BF16)
                hr = (SBring(S1, 1, [128, 128], F32), SBring(S1, 2, [128, 8], F32), SBring(S1, 2, [128, 8], F32))
                rr = (SBring(S1, 1, [128, 128], F32), SBring(S1, 1, [128, 128], F32))
                def p1_front(j):
                    xa = xr.next()
                    ld(xa[:], xall[j * 128:(j + 1) * 128, :], [xa], key="xa%d" % (j % 2))
                    return rms_part1(nb, xa, g1b)

                snfr = SBring(S1, 2, [128, 512], F32, "snf"); snbr = SBring(S1, 2, [128, 512], F32, "snb")
                stB = {}

                def b_ew(dr):
                    jb, sn_b = stB["j"], stB["sn" + dr]
                    mk = mpr if dr == "f" else mpo
                    k = kr_.next(); g = gr.next()
                    stt(k[:], sn_b[:], mk[:, jb:jb + 1], oml[dr][:], ALU.mult, ALU.mult, [sn_b, mk, oml[dr]], [k])
                    act(g[:], k[:], AF.Ln, [k], [g], scale=-1.0, bias=1.0)
                    stB["k" + dr], stB["g" + dr] = k, g

                def b_suf(dr):
                    k, g = stB["k" + dr], stB["g" + dr]
                    ps_s = psF.next()
                    ti = T_SUFF128 if dr == "f" else T_SUFB128
                    mm(ps_s[:], trif[:, ti, :], g[:], [trif, g], [ps_s])
                    E = Er.next(); kc = kcr.next()
                    act(E[:], ps_s[:], AF.Exp, [ps_s], [E])
                    tt("dve", kc[:], k[:], E[:], ALU.mult, [k, E], [kc])
                    stB["kc" + dr] = kc

                def b_p2(dr):
                    v_b, g, kc = stB["v"], stB["g" + dr], stB["kc" + dr]
                    dec = decr.next()
                    ps_d = psF.next()
                    for hh in range(4):
                        mm(ps_d[:, 2 * hh:2 * hh + 2], g[:, hh * 128:(hh + 1) * 128], onesf[:, 0:2], [g, onesf], [ps_d])
                    act(dec[:], ps_d[:, 0:8], AF.Exp, [ps_d], [dec])
                    ps_U = psF.next()
                    for hh in range(4):
                        mm(ps_U[:, hh * 128:(hh + 1) * 128], kc[:, hh * 128:(hh + 1) * 128], v_b[:, hh * 128:(hh + 1) * 128], [kc, v_b], [ps_U])
                    Sd = Sin[dr]
                    for hh in range(4):
                        if dr == "f":
                            stt(Sd[:, hh, :], Sd[:, hh, :], dec[:, 2 * hh:2 * hh + 1], ps_U[:, hh * 128:(hh + 1) * 128],
                                ALU.mult, ALU.add, [Sd, dec, ps_U], [Sd])
                        else:
                            stt(Sd[:, hh, :], ps_U[:, hh * 128:(hh + 1) * 128], Pd[:, hh:hh + 1], Sd[:, hh, :],
                                ALU.mult, ALU.add, [Sd, Pd, ps_U], [Sd])
                    if dr == "b":
                        tt("dve", Pd[:], Pd[:], dec[:, 0:8:2], ALU.mult, [Pd, dec], [Pd])

                xn_next = p1_front(0)
                for j in range(NA + 1):
                    haveA = j < NA
                    haveB = j >= 1
                    if haveB:
                        b_ew("f"); b_ew("b")
                    if haveA:
                        ck = ckr.next(); sk = skr.next()
                        ld(ck[:], cosk[j * 128:(j + 1) * 128, :], [ck], key="ck%d" % (j % 2))
                        ld(sk[:], sink[j * 128:(j + 1) * 128, :], [sk], key="sk%d" % (j % 2))
                        uT = rms_part2(nb, xn_next)
                        if j + 1 < NA:
                            xn_next = p1_front(j + 1)
                        ps_v = psF.next(); proj_tok(ps_v, uT, wA, 0, 512)
                        v = vr.next(); cp("dve", v[:], ps_v[:], [ps_v], [v])
                    if haveB:
                        b_suf("f"); b_suf("b")
                    if haveA:
                        ps_ff = psF.next(); proj_tok(ps_ff, uT, wA, 512, 512)
                        snf = snfr.next()
                        act_sigmoid(snf[:], ps_ff[:], [ps_ff], snf, True)
                    if haveB:
                        b_p2("f")
                    if haveA:
                        ps_fb = psF.next(); proj_tok(ps_fb, uT, wA, 1024, 512)
                        snb = snbr.next()
                        act_sigmoid(snb[:], ps_fb[:], [ps_fb], snb, True)
                    if haveB:
                        b_p2("b")
                    if haveA:
                        ps_kv = psF.next(); proj_tok(ps_kv, uT, wA, 1536, 256)
                        kraw = kraw_r.next(); kn = kn_r.next(); krb = krb_r.next()
                        cp("act", kraw[:], ps_kv[:, 0:128], [ps_kv], [kraw])
                        P.op("act", lambda h, j=j, ps_kv=ps_kv: h.copy(out=V1[:, j, 0, 0:64], in_=ps_kv[:, 128:192]), [ps_kv], [V1])
                        P.op("act", lambda h, j=j, ps_kv=ps_kv: h.copy(out=V1[:, j, 1, 64:128], in_=ps_kv[:, 192:256]), [ps_kv], [V1])
                        head_rms(hr, kraw, 2, 64, kgb, kn)
                        t1, t2 = rope_parts(rr, kn, 128, ck, sk)
                        tt("dve", krb[:], t1[:], t2[:], ALU.add, [t1, t2], [krb])
                        pT = psB.next()
                        tr(pT[:, 0, :], krb[:], identb[:], [krb, identb], [pT])
                        cp("act", KT[:, j * 128:(j + 1) * 128], pT[:, 0, :], [pT], [KT])
                    if haveA:
                        stB = {"j": j, "v": v, "snf": snf, "snb": snb}
                P.barrier()

            if debug:
                out_deps.append(ld(DBG_S[:, 0, :], Sin["f"][:].rearrange("p a b -> p (a b)"), [r_dbg], [Sin["f"]]))
                out_deps.append(ld(DBG_S[:, 1, :], Sin["b"][:].rearrange("p a b -> p (a b)"), [r_dbg], [Sin["b"]]))

            if stop == "P1":
                raise _Stop()
            with contextlib.ExitStack() as S3:
                pr = SBring(S3, 4, [128, 512], BF16, "p")
                ocr = SBring(S3, 2, [128, 512], F32); rrow = SBring(S3, 2, [128, 512], F32)
                psS = Ring(psF.bufs[0:3]); psO = Ring(psF.bufs[3:5]); psX = Ring(psF.bufs[5:6])
                qblocks = [(q0, min(512, NEXT - q0)) for q0 in range(0, NEXT, 512)]
                items = [(hd_, q0, nq, kt) for hd_ in range(8) for (q0, nq) in qblocks for kt in range(NA)]
                PF = 2

                def issue_s(it):
                    hd_, q0, nq, kt = it
                    kv, jj = hd_ // 4, hd_ % 4
                    pl = slice(kv * 64, (kv + 1) * 64)
                    ps_s = psS.next()
                    mm(ps_s[:, 0:nq], KT[:, kt * 128:(kt + 1) * 128], QT[:, hd_, q0:q0 + nq], [KT, QT], [ps_s])
                    return ps_s

                pend = [issue_s(items[i]) for i in range(min(PF, len(items)))]
                ps_o = None
                for i, (hd_, q0, nq, kt) in enumerate(items):
                    kv, jj = hd_ // 4, hd_ % 4
                    pl = slice(kv * 64, (kv + 1) * 64)
                    dr_ = 64 if kv == 0 else 0
                    ps_s = pend.pop(0)
                    if i + PF < len(items):
                        pend.append(issue_s(items[i + PF]))
                    if kt == 0:
                        ps_o = psO.next()
                    p = pr.next()
                    act(p[:, 0:nq], ps_s[:, 0:nq], AF.Exp, [ps_s], [p], scale=0.125)
                    mm(ps_o[:, 0:nq], V1[:, kt, kv, :], p[:, 0:nq], [V1, p], [ps_o], start=(kt == 0), stop=(kt == NA - 1))
                    if kt == NA - 1:
                        oc = ocr.next(); rw = rrow.next()
                        cp("act", oc[:, 0:nq], ps_o[:, 0:nq], [ps_o], [oc])
                        act(rw[dr_:dr_ + 1, 0:nq], oc[dr_:dr_ + 1, 0:nq], AF.Ln, [oc], [rw])
                        act(rw[dr_:dr_ + 1, 0:nq], rw[dr_:dr_ + 1, 0:nq], AF.Exp, [rw], [rw], scale=-1.0)
                        ps_b = psX.next()
                        mm(ps_b[:, 0:nq], onesf[dr_:dr_ + 1, :], rw[dr_:dr_ + 1, 0:nq], [onesf, rw], [ps_b])
                        tt("dve", obT[pl, jj, q0:q0 + nq], oc[pl, 0:nq], ps_b[pl, 0:nq], ALU.mult, [oc, ps_b], [obT])
                P.barrier()
            SKV.close()
            SQ.close()
            if debug:
                out_deps.append(ld(DBG_OB, obT[:], [r_dbg], [obT]))

            def hgrn_sweep(dr, st, merge):
                PJ = SBring(st, 2, [128, 512], F32, "hq"); PI = SBring(st, 2, [128, 512], F32, "hi")
                PFr = SBring(st, 2, [128, 512], F32, "hf")
                qsr = SBring(st, 1, [128, 512], BF16); snr = SBring(st, 1, [128, 512], F32); kr_ = SBring(st, 1, [128, 512], F32)
                kbr = SBring(st, 1, [128, 512], BF16); gr = SBring(st, 1, [128, 512], F32); vr = SBring(st, 2, [128, 512], BF16)
                E1r = SBring(st, 1, [128, 4, 128], F32); E2r = SBring(st, 1, [128, 4, 128], F32); E3r = SBring(st, 1, [128, 512], F32)
                der = SBring(st, 2, [128, 4, 2], F32)
                qdr = SBring(st, 2, [128, 4, 128], BF16); kdr = SBring(st, 1, [128, 4, 128], BF16); kcr = SBring(st, 2, [128, 512], BF16)
                atr = SBring(st, 1, [128, 4, 128], BF16); oir = SBring(st, 2, [128, 512], F32); oor = SBring(st, 2, [128, 512], F32)
                S32r = SBring(st, 3, [128, 4, 128], F32); Sbfr = SBring(st, 3, [128, 4, 128], BF16)
                qsgr = SBring(st, 1, [128, 512], F32)
                ti_inc = T_INCF if dr == "f" else T_INCB
                ti_suf = T_SUFF64 if dr == "f" else T_SUFB64
                c_hf = 1024 if dr == "f" else 1536
                order = [0, 1] if dr == "f" else [1, 0]
                rows = [slice(0, 64), slice(64, 128)]
                if merge:
                    obr = SBring(st, 2, [128, 512], F32, "ob"); PGr = SBring(st, 2, [128, 512], F32, "hg")
                    GAr = SBring(st, 2, [128, 512], F32, "ga"); GBr = SBring(st, 2, [128, 512], F32, "gb")
                    xer = SBring(st, 1, [128, D], F32, "xe2")
                    hr = (SBring(st, 1, [128, 512], F32), SBring(st, 2, [128, 8], F32), SBring(st, 2, [128, 8], F32))
                    onr = SBring(st, 1, [128, 512], F32); sgr = SBring(st, 1, [128, 512], F32); oar = SBring(st, 1, [128, 512], BF16)
                    oaTr = SBring(st, 1, [128, 4, 128], BF16)
                    m1r = SBring(st, 1, [128, 512], F32); m2r = SBring(st, 1, [128, 512], F32); mgr = SBring(st, 1, [128, D], BF16)
                    mgTr = SBring(st, 1, [128, KC, 128], BF16); htr = SBring(st, 1, [128, D], F32)
                    nb2 = norm_bufs(st)
                    Wa = SB(st, [128, 4, D], BF16); ldw(Wa[:], w_a.rearrange("(c p) n -> p c n", p=128), [Wa])
                    Wb = SB(st, [128, 4, D], BF16)
                    wb64 = w_b.rearrange("(c p) n -> p c n", p=64)
                    ldw(Wb[0:64, :, :], wb64[:, 0:4, :], [Wb])
                    ldw(Wb[64:128, :, :], wb64[:, 4:8, :], [Wb])
                    Wo = SB(st, [128, KC, D], BF16); ldw(Wo[:], w_out.rearrange("(c p) n -> p c n", p=128), [Wo])
                S32 = S32r.next(); Sbf = Sbfr.next()
                cp("dve", S32[:], Sin[dr][:], [Sin[dr]], [S32])
                cp("act", Sbf[:], Sin[dr][:], [Sin[dr]], [Sbf])
                st_cur = [S32, Sbf]
                tiles = list(range(NE) if dr == "f" else range(NE - 1, -1, -1))

                def frontA(t):
                    rsl = slice(t * 128, (t + 1) * 128)
                    hq = PJ.next(); hi = PI.next(); hf = PFr.next()
                    ld(hq[:], PROJ[rsl, 0:512], [hq], [r_PROJ[t]], key="%shq%d" % (dr, t % 2))
                    ld(hi[:], PROJ[rsl, 512:1024], [hi], [r_PROJ[t]], key="%shi%d" % (dr, t % 2))
                    ld(hf[:], PROJ[rsl, c_hf:c_hf + 512], [hf], [r_PROJ[t]], key="%shf%d" % (dr, t % 2))
                    qs = qsr.next(); sn = snr.next(); k = kr_.next(); kb = kbr.next(); g = gr.next(); v = vr.next()
                    act_sigmoid(sn[:], hf[:], [hf], sn, True)
                    tt("dve", k[:], sn[:], oml[dr][:], ALU.mult, [sn, oml[dr]], [k])
                    tt("dve", kb[:], sn[:], oml[dr][:], ALU.mult, [sn, oml[dr]], [kb])
                    act(g[:], k[:], AF.Ln, [k], [g], scale=-1.0, bias=1.0)
                    qsg = qsgr.next()
                    act_sigmoid(qsg[:], hq[:], [hq], qsg, False)
                    tt("dve", qs[:], hq[:], qsg[:], ALU.mult, [hq, qsg], [qs])
                    cp("dve", v[:], hi[:], [hi], [v])
                    pq = psB.next()
                    for hh in range(4):
                        tr(pq[:, hh, :], qs[:, hh * 128:(hh + 1) * 128], identb[:], [qs, identb], [pq])
                        tr(pq[:, 4 + hh, :], kb[:, hh * 128:(hh + 1) * 128], identb[:], [kb, identb], [pq])
                    ps_bT = psF.next(); ps_sf = psF.next(); ps_dc = psF.next()
                    for hh in range(4):
                        mm(ps_bT[:, hh * 128:(hh + 1) * 128], g[:, hh * 128:(hh + 1) * 128], trif[:, ti_inc, :], [g, trif], [ps_bT])
                    mm(ps_sf[:], trif[:, ti_suf, :], g[:], [trif, g], [ps_sf])
                    for hh in range(4):
                        mm(ps_dc[:, hh * 2:hh * 2 + 2], g[:, hh * 128:(hh + 1) * 128], cindf[:], [g, cindf], [ps_dc])
                    return dict(t=t, rsl=rsl, k=k, v=v, pq=pq, ps_bT=ps_bT, ps_sf=ps_sf, ps_dc=ps_dc)

                def frontB(fa):
                    k, v, pq, ps_bT, ps_sf, ps_dc = fa["k"], fa["v"], fa["pq"], fa["ps_bT"], fa["ps_sf"], fa["ps_dc"]
                    E1 = E1r.next(); E2 = E2r.next(); E3 = E3r.next(); de = der.next()
                    act(E1[:].rearrange("p a b -> p (a b)"), ps_bT[:], AF.Exp, [ps_bT], [E1])
                    act(E2[:].rearrange("p a b -> p (a b)"), ps_bT[:], AF.Exp, [ps_bT], [E2], scale=-1.0)
                    act(E3[:], ps_sf[:], AF.Exp, [ps_sf], [E3])
                    act(de[:].rearrange("p a b -> p (a b)"), ps_dc[:, 0:8], AF.Exp, [ps_dc], [de])
                    qd = qdr.next(); kd = kdr.next(); kc = kcr.next()
                    tt("dve", qd[:], pq[:, 0:4, :], E1[:], ALU.mult, [pq, E1], [qd])
                    tt("dve", kd[:], pq[:, 4:8, :], E2[:], ALU.mult, [pq, E2], [kd])
                    tt("dve", kc[:], k[:], E3[:], ALU.mult, [k, E3], [kc])
                    ps_A = psF.next()
                    for hh in range(4):
                        mm(ps_A[:, hh * 128:(hh + 1) * 128], kd[:, hh, :], qd[:, hh, :], [kd, qd], [ps_A])
                    at = atr.next()
                    tt("dve", at[:].rearrange("p a b -> p (a b)"), ps_A[:], mask4[dr][:].rearrange("p a b -> p (a b)"),
                       ALU.mult, [ps_A, mask4[dr]], [at])
                    ps_o = psF.next()
                    for hh in range(4):
                        mm(ps_o[:, hh * 128:(hh + 1) * 128], at[:, hh, :], v[:, hh * 128:(hh + 1) * 128], [at, v], [ps_o])
                    oi = oir.next()
                    cp("act", oi[:], ps_o[:], [ps_o], [oi])
                    return dict(t=fa["t"], rsl=fa["rsl"], qd=qd, kc=kc, v=v, de=de, oi=oi)

                def back_chunk(cur, ps_i, cc):
                    qd, kc, v, de = cur["qd"], cur["kc"], cur["v"], cur["de"]
                    S32, Sbf = st_cur[0], st_cur[1]
                    rws = rows[cc]
                    for hh in range(4):
                        mm(ps_i[rws, hh * 128:(hh + 1) * 128], qd[:, hh, rws], Sbf[:, hh, :], [qd, Sbf], [ps_i])
                    ps_U = psF.next()
                    for hh in range(4):
                        mm(ps_U[:, hh * 128:(hh + 1) * 128], kc[rws, hh * 128:(hh + 1) * 128], v[rws, hh * 128:(hh + 1) * 128],
                           [kc, v], [ps_U])
                    S32n = S32r.next(); Sbfn = Sbfr.next()
                    for hh in range(4):
                        stt(S32n[:, hh, :], S32[:, hh, :], de[:, hh, cc:cc + 1], ps_U[:, hh * 128:(hh + 1) * 128],
                            ALU.mult, ALU.add, [S32, de, ps_U], [S32n])
                    cp("act", Sbfn[:], S32n[:], [S32n], [Sbfn])
                    st_cur[0], st_cur[1] = S32n, Sbfn

                fr_next = frontB(frontA(tiles[0]))
                for ti_, t in enumerate(tiles):
                    cur = fr_next
                    nxt = tiles[ti_ + 1] if ti_ + 1 < len(tiles) else None
                    rsl, oi = cur["rsl"], cur["oi"]
                    ps_i = psF.next(); oo = oor.next()
                    back_chunk(cur, ps_i, order[0])
                    fa = frontA(nxt) if nxt is not None else None
                    back_chunk(cur, ps_i, order[1])
                    tt("dve", oo[:], oi[:], ps_i[:], ALU.add, [oi, ps_i], [oo])
                    if fa is not None:
                        fr_next = frontB(fa)
                    if not merge:
                        ld(OBWD[rsl, :], oo[:], [r_OBWD[t]], [oo], key="obw%d" % (t % 2))
                        continue
                    ob = obr.next(); hg = PGr.next(); xe = xer.next()
                    ld(ob[:], OBWD[rsl, :], [ob], [r_OBWD[t]], key="lob%d" % (t % 2))
                    ld(hg[:], PROJ[rsl, 2048:2560], [hg], [r_PROJ[t]], key="lhg%d" % (t % 2))
                    ld(xe[:], xext[rsl, :], [xe], key="lxe")
                    tt("pool", oo[:], oo[:], ob[:], ALU.add, [oo, ob], [oo])
                    on = onr.next(); sg = sgr.next(); oa = oar.next()
                    head_rms(hr, oo, 4, 128, onb, on)
                    act_sigmoid(sg[:], hg[:], [hg], sg, False)
                    tt("dve", sg[:], sg[:], hg[:], ALU.mult, [sg, hg], [sg])
                    tt("dve", oa[:], on[:], sg[:], ALU.mult, [on, sg], [oa])
                    if debug:
                        out_deps.append(ld(DBG_OA[rsl, :], on[:], [r_dbg], [on], key="dboa"))
                    pT = psB.next(); oaT = oaTr.next()
                    for hh in range(4):
                        tr(pT[:, hh, :], oa[:, hh * 128:(hh + 1) * 128], identb[:], [oa, identb], [pT])
                    cp("act", oaT[:], pT[:, 0:4, :], [pT], [oaT])
                    mg = mgr.next()
                    for hf_ in range(2):
                        cs = slice(hf_ * 512, (hf_ + 1) * 512)
                        ga = GAr.next(); gb_ = GBr.next(); m1 = m1r.next(); m2 = m2r.next()
                        ld(ga[:], PROJ[rsl, 3072 + hf_ * 512:3072 + (hf_ + 1) * 512], [ga], [r_PROJ[t]], key="lga%d" % hf_)
                        ld(gb_[:], PROJ[rsl, 4096 + hf_ * 512:4096 + (hf_ + 1) * 512], [gb_], [r_PROJ[t]], key="lgb%d" % hf_)
                        act_sigmoid(ga[:], ga[:], [ga], ga, False)
                        act_sigmoid(gb_[:], gb_[:], [gb_], gb_, False)
                        ps_a = psF.next()
                        for c in range(4):
                            mm(ps_a[:], oaT[:, c, :], Wa[:, c, cs], [oaT, Wa], [ps_a], start=(c == 0), stop=(c == 3))
                        tt("dve", m1[:], ps_a[:], ga[:], ALU.mult, [ps_a, ga], [m1])
                        ps_b2 = psF.next()
                        for jj in range(4):
                            mm(ps_b2[:], obT[:, jj, rsl], Wb[:, jj, cs], [obT, Wb], [ps_b2], start=(jj == 0), stop=(jj == 3))
                        tt("dve", m2[:], ps_b2[:], gb_[:], ALU.mult, [ps_b2, gb_], [m2])
                        tt("pool", mg[:, cs], m1[:], m2[:], ALU.add, [m1, m2], [mg])
                    pT = psB.next(); mgT = mgTr.next()
                    for c in range(KC):
                        tr(pT[:, c, :], mg[:, c * 128:(c + 1) * 128], identb[:], [mg, identb], [pT])
                    cp("act", mgT[:], pT[:], [pT], [mgT])
                    ht = htr.next()
                    for hf_ in range(2):
                        cs = slice(hf_ * 512, (hf_ + 1) * 512)
                        ps_h = psF.next()
                        for c in range(KC):
                            mm(ps_h[:], mgT[:, c, :], Wo[:, c, cs], [mgT, Wo], [ps_h], start=(c == 0), stop=(c == KC - 1))
                        tt("dve", ht[:, cs], ps_h[:], xe[:, cs], ALU.add, [ps_h, xe], [ht])
                    lo = max(0, 64 - t * 128); hi_ = min(128, OWN + 64 - t * 128)
                    if hi_ > lo:
                        y0 = t * 128 - 64 + lo
                        yres = [r_y[i] for i in sorted(set([y0 // 128, (y0 + hi_ - lo - 1) // 128]))]
                        ld(y[y0:y0 + (hi_ - lo), :], ht[lo:hi_, :], yres, [ht], key="yh")
                    if debug:
                        out_deps.append(ld(DBG_H[rsl, :], ht[:], [r_dbg], [ht], key="dbh"))
                    hnT = rmsnorm_T(nb2, ht, g2b)
                    ld(HNT.rearrange("(c p) n -> p c n", p=128)[:, :, rsl], hnT[:], [r_HNT[t]], [hnT], key="hnt%d" % (t % 2))

            if stop == "P3":
                raise _Stop()
            with contextlib.ExitStack() as S4:
                hgrn_sweep("b", S4, False)
                P.barrier()
            if stop == "P4":
                raise _Stop()
            with contextlib.ExitStack() as S5:
                hgrn_sweep("f", S5, True)
                P.barrier()
            if stop == "P5":
                raise _Stop()

            with contextlib.ExitStack() as S6:
                Wd = SB(S6, [128, 22, D], BF16)
                ldw(Wd[:, 0:11, :], w_down.rearrange("(c p) n -> p c n", p=128)[:, 0:11, :], [Wd])
                ldw(Wd[:, 11:22, :], w_down.rearrange("(c p) n -> p c n", p=128)[:, 11:22, :], [Wd])
                hbr = SBring(S6, 2, [128, KC, 514], BF16, "hb")
                wur = SBring(S6, 4, [128, KC, 2, 128], BF16, "wu")
                upr = SBring(S6, 4, [128, 514], F32, "up")
                c1r = SBring(S6, 1, [128, 512], F32); c2r = SBring(S6, 1, [128, 512], F32)
                g1r = SBring(S6, 1, [128, 512], F32); g2r = SBring(S6, 1, [128, 512], F32); sgr = SBring(S6, 1, [128, 512], F32)
                aTr = SBring(S6, 1, [128, 22, 512], BF16, "aT")
                hrr = SBring(S6, 2, [128, D], F32, "hres"); outr = SBring(S6, 2, [128, D], F32, "out")
                nblk = OWN // 512
                for b in range(nblk):
                    c0 = 64 + 512 * b
                    hb = hbr.next()
                    hres_list = [r_HNT[i] for i in range((c0 - 1) // 128, (c0 + 512) // 128 + 1)]
                    ld(hb[:], HNT.rearrange("(c p) n -> p c n", p=128)[:, :, c0 - 1:c0 + 513], [hb], hres_list, key="hb%d" % (b % 2))
                    if b == 0:
                        ts("dve", hb[:, :, 0:1], hb[:, :, 0:1], cvl[:, 0:1], None, ALU.mult, None, [hb, cvl], [hb])
                    if b == nblk - 1:
                        ts("dve", hb[:, :, 513:514], hb[:, :, 513:514], cvl[:, 1:2], None, ALU.mult, None, [hb, cvl], [hb])
                    aT = aTr.next()
                    for c in range(22):
                        wuk = wur.i % 4
                        wu = wur.next()
                        ldw(wu[:, :, 0, :], w_up[c], [wu], key="wu%da" % wuk)
                        ldw(wu[:, :, 1, :], w_up[22 + c], [wu], key="wu%db" % wuk)
                        res = []
                        for vg in range(2):
                            ps_m = psF.next(); ps_e = psF.next()
                            for kc_ in range(KC):
                                mm(ps_m[:], wu[:, kc_, vg, :], hb[:, kc_, 1:513], [wu, hb], [ps_m], start=(kc_ == 0), stop=(kc_ == KC - 1))
                            for kc_ in range(KC):
                                mm(ps_e[:, 0:2], wu[:, kc_, vg, :], hb[:, kc_, 0:514:513], [wu, hb], [ps_e], start=(kc_ == 0), stop=(kc_ == KC - 1))
                            up = upr.next()
                            cp("act", up[:, 1:513], ps_m[:], [ps_m], [up])
                            cp("dve", up[:, 0:514:513], ps_e[:, 0:2], [ps_e], [up])
                            ch = c + 22 * vg
                            a1 = (c1r if vg == 0 else g1r).next(); a2 = (c2r if vg == 0 else g2r).next()
                            act(a1[:], ps_m[:], AF.Identity, [ps_m, cwt, cbt], [a1], scale=cwt[:, ch, 1:2], bias=cbt[:, ch:ch + 1])
                            stt(a2[:], up[:, 0:512], cwt[:, ch, 0:1], a1[:], ALU.mult, ALU.add, [up, cwt, a1], [a2])
                            stt(a1[:], up[:, 2:514], cwt[:, ch, 2:3], a2[:], ALU.mult, ALU.add, [up, cwt, a2], [a1])
                            res.append(a1)
                        sg = sgr.next()
                        act(sg[:], res[1][:], AF.Silu, [res[1]], [sg])
                        tt("dve", aT[:, c, :], sg[:], res[0][:], ALU.mult, [sg, res[0]], [aT])
                    for tt_ in range(4):
                        ot_i = b * 4 + tt_
                        row0 = ot_i * 128
                        hres = hrr.next(); ot = outr.next()
                        ld(hres[:], y[row0:row0 + 128, :], [hres], [r_y[ot_i]], key="hres%d" % (tt_ % 2))
                        for hf_ in range(2):
                            cs = slice(hf_ * 512, (hf_ + 1) * 512)
                            ps_f2 = psF.next()
                            for c in range(22):
                                mm(ps_f2[:], aT[:, c, tt_ * 128:(tt_ + 1) * 128], Wd[:, c, cs], [aT, Wd], [ps_f2], start=(c == 0), stop=(c == 21))
                            tt("dve", ot[:, cs], ps_f2[:], hres[:, cs], ALU.add, [ps_f2, hres], [ot])
                        out_deps.append(ld(y[row0:row0 + 128, :], ot[:], [r_y[ot_i]], [ot], key="yo%d" % (tt_ % 2)))
        except _Stop:
            P.barrier()
            for st_ in reversed(open_stacks):
                st_.close()
        P.build(final_deps=out_deps)
    return nc


def _rope_tables(pos_ids):
    pos = np.asarray(pos_ids)
    pr = (pos // 64).astype(np.float32)
    pc = (pos % 64).astype(np.float32)
    inv = (np.float32(10000.0) ** (-np.arange(0, 32, 2, dtype=np.float32) / np.float32(32))).astype(np.float32)

    def tab(p):
        ang = (p[:, None] * inv[None, :]).astype(np.float32)
        ang = np.concatenate([ang, ang], axis=-1)
        return np.cos(ang).astype(np.float32), np.sin(ang).astype(np.float32)

    cr, sr = tab(pr)
    cc, sc = tab(pc)
    C = np.concatenate([cr, cc], axis=-1)
    S = np.concatenate([sr, sc], axis=-1)
    sign = np.tile(np.concatenate([-np.ones(16, np.float32), np.ones(16, np.float32)]), 2)
    return C, (S * sign[None, :]).astype(np.float32)


def _consts():
    s = np.arange(128)[:, None]
    c = np.arange(128)[None, :]
    same = (s // 64) == (c // 64)
    mats = [same & (s <= c), same & (s >= c), same & (s > c), same & (s < c), s > c, s < c]
    tri = np.stack([m.astype(np.float32) for m in mats], axis=1)
    cind = np.stack([(np.arange(128) < 64), (np.arange(128) >= 64)], axis=1).astype(np.float32)
    return np.ascontiguousarray(tri), cind, np.eye(128, dtype=np.float32)


_CACHE = {}


def run(inputs, SEQ, debug=False, stop=None):
    f = lambda a: np.ascontiguousarray(np.asarray(a, dtype=np.float32))
    x = f(inputs["x"])
    B = x.shape[0]
    NA = SEQ // 128
    OWN = SEQ // 4
    NE = OWN // 128 + 1
    NEXT = NE * 128
    tri, cind, ident = _consts()
    Ck, Sk = _rope_tables(np.arange(SEQ))
    shared = {
        "cosk": np.ascontiguousarray(np.tile(Ck, (1, 2))), "sink": np.ascontiguousarray(np.tile(Sk, (1, 2))),
        "tri": tri, "cind": cind, "ident": ident,
        "w_in": f(inputs["w_in"][0]), "w_a": f(inputs["w_branch_a"][0]), "w_b": f(inputs["w_branch_b"][0]),
        "w_out": f(inputs["w_out"][0]), "w_down": f(inputs["w_down"][0]),
        "w_up": np.ascontiguousarray(f(inputs["w_up"][0]).reshape(KC, 128, 44, 128).transpose(2, 1, 0, 3)),
        "g1": f(inputs["norm1_g"][0:1]), "g2": f(inputs["norm2_g"][0:1]),
        "lbf": f(inputs["hg_lb_fwd"]), "lbb": f(inputs["hg_lb_bwd"]),
        "onorm": np.ascontiguousarray(np.tile(f(inputs["hg_onorm_g"][0:1]), (1, 4))),
        "qg": np.ascontiguousarray(np.tile(f(inputs["q_norm_g"][0:1]), (1, 8))),
        "kg": np.ascontiguousarray(np.tile(f(inputs["k_norm_g"][0:1]), (1, 2))),
        "cw": np.ascontiguousarray(f(inputs["conv_w"][0]).reshape(3, 44, 128).transpose(2, 1, 0)),
        "cb": np.ascontiguousarray(f(inputs["conv_b"][0]).reshape(44, 128).T),
    }
    in_maps = []
    for c in range(8):
        b, r = c // 4, c % 4
        s0 = r * OWN
        e0 = s0 - 64
        pos = np.arange(e0, e0 + NEXT)
        valid = (pos >= 0) & (pos < SEQ)
        xe = np.zeros((NEXT, D), np.float32)
        xe[valid] = x[b, pos[valid]]
        Cq, Sq = _rope_tables(np.clip(pos, 0, SEQ - 1))
        tok = np.arange(SEQ).reshape(NA, 128).T
        m = dict(shared)
        m.update({
            "xall": x[b], "xext": xe,
            "cosq": np.ascontiguousarray(np.tile(Cq, (1, 8))), "sinq": np.ascontiguousarray(np.tile(Sq, (1, 8))),
            "mpre": (tok < e0).astype(np.float32), "mpost": (tok >= e0 + NEXT).astype(np.float32),
            "cval": np.ascontiguousarray(np.tile(np.array([[float(s0 > 0), float(s0 + OWN < SEQ)]], np.float32), (128, 1))),
        })
        in_maps.append(m)
    key = (SEQ, debug, stop)
    if key not in _CACHE:
        _CACHE[key] = build_program(SEQ, debug, stop)
    nc = _CACHE[key]
    res = run_bass_kernel_spmd(nc, in_maps, core_ids=list(range(8)))
    out = np.zeros((B, SEQ, D), np.float32)
    for c in range(8):
        b, r = c // 4, c % 4
        out[b, r * OWN:(r + 1) * OWN] = res.results[c]["y"]
    return out, res


def kernel(**inputs):
    out, _ = run(inputs, 8192)
    return out
```
